# Optimizing a Trainium2 kernel written in Bass

```python
import jax, jax.numpy as jnp
from jax import lax
import numpy as np

D_MODEL = 1024
BATCH = 8
SEQ = 2048
DEPTH = 1

CONV_WIDTH = D_MODEL
CONV_GROUPS = 8
LRU_WIDTH = D_MODEL
LRU_HEADS = 4
LRU_HEAD_DIM = LRU_WIDTH // LRU_HEADS
SHORT_CONV_K = 3
LRU_CONV_K = 4
FFN_CONV_K = 3
D_FF = 3 * D_MODEL
LRU_C = 8.0
RMS_EPS = 1e-6
IN_COLS = 3 * CONV_WIDTH + 2 * LRU_WIDTH + 2 * D_MODEL
SPLITS = (CONV_WIDTH, 2 * CONV_WIDTH, 3 * CONV_WIDTH,
          3 * CONV_WIDTH + LRU_WIDTH, 3 * CONV_WIDTH + 2 * LRU_WIDTH,
          3 * CONV_WIDTH + 2 * LRU_WIDTH + D_MODEL)

kernel_name = "hybrid_shortconv_rglru_convffn_sandwich"


def rmsnorm(x, g):
    xf = x.astype(jnp.float32)
    y = xf * lax.rsqrt(jnp.mean(xf * xf, axis=-1, keepdims=True) + RMS_EPS)
    return (y * g.astype(jnp.float32)).astype(x.dtype)


def causal_dwconv(x, w, b=None):
    k_width = w.shape[0]
    s = x.shape[1]
    xp = jnp.pad(x, ((0, 0), (k_width - 1, 0), (0, 0)))
    y = xp[:, 0:s] * w[0]
    for k in range(1, k_width):
        y = y + xp[:, k:k + s] * w[k]
    if b is not None:
        y = y + b
    return y


def rg_lru(x, w_a, b_a, w_x, b_x, lam):
    bsz, s, w = x.shape
    xf = x.astype(jnp.float32)
    xh = xf.reshape(bsz, s, LRU_HEADS, LRU_HEAD_DIM)
    r = jax.nn.sigmoid(jnp.einsum("bshi,hij->bshj", xh, w_a.astype(jnp.float32)) + b_a.astype(jnp.float32)).reshape(bsz, s, w)
    i = jax.nn.sigmoid(jnp.einsum("bshi,hij->bshj", xh, w_x.astype(jnp.float32)) + b_x.astype(jnp.float32)).reshape(bsz, s, w)
    log_a = LRU_C * r * jax.nn.log_sigmoid(lam.astype(jnp.float32))
    a = jnp.exp(log_a)
    mult = jnp.sqrt(-jnp.expm1(2.0 * log_a))
    first = (jnp.arange(s) == 0)[None, :, None]
    mult = jnp.where(first, 1.0, mult)
    u = mult * (i * xf)

    def combine(left, right):
        a1, b1 = left
        a2, b2 = right
        return a1 * a2, a2 * b1 + b2

    _, h = lax.associative_scan(combine, (a, u), axis=1)
    return h.astype(x.dtype)


def setup_inputs(seed: int = 0) -> dict:
    key = jax.random.key(seed)
    ks = jax.random.split(key, 24)
    f32 = jnp.float32

    def nrm(k, shape, fan_in):
        return jax.random.normal(k, shape, f32) * (fan_in ** -0.5)

    def gain(k, n):
        return 1.0 + 0.05 * jax.random.normal(k, (DEPTH, n), f32)

    u = jax.random.uniform(ks[14], (DEPTH, LRU_WIDTH), f32, 0.9, 0.999)
    a0 = u ** (1.0 / LRU_C)
    lam = jnp.log(a0) - jnp.log1p(-a0)

    return {
        "x": jax.random.normal(ks[0], (BATCH, SEQ, D_MODEL), f32),
        "norm_mix_pre": gain(ks[1], D_MODEL),
        "norm_mix_post": gain(ks[2], D_MODEL),
        "norm_ffn_pre": gain(ks[3], D_MODEL),
        "norm_ffn_post": gain(ks[4], D_MODEL),
        "w_in": nrm(ks[5], (DEPTH, D_MODEL, IN_COLS), D_MODEL),
        "conv_short_w": nrm(ks[6], (DEPTH, SHORT_CONV_K, CONV_WIDTH), SHORT_CONV_K),
        "w_conv_branch": nrm(ks[7], (DEPTH, CONV_WIDTH, D_MODEL), CONV_WIDTH),
        "lru_conv_w": nrm(ks[8], (DEPTH, LRU_CONV_K, LRU_WIDTH), LRU_CONV_K),
        "lru_conv_b": 0.02 * jax.random.normal(ks[9], (DEPTH, LRU_WIDTH), f32),
        "lru_wa": nrm(ks[10], (DEPTH, LRU_HEADS, LRU_HEAD_DIM, LRU_HEAD_DIM), LRU_HEAD_DIM),
        "lru_ba": 0.02 * jax.random.normal(ks[11], (DEPTH, LRU_HEADS, LRU_HEAD_DIM), f32),
        "lru_wx": nrm(ks[12], (DEPTH, LRU_HEADS, LRU_HEAD_DIM, LRU_HEAD_DIM), LRU_HEAD_DIM),
        "lru_bx": 0.02 * jax.random.normal(ks[13], (DEPTH, LRU_HEADS, LRU_HEAD_DIM), f32),
        "lru_lambda": lam,
        "w_lru_branch": nrm(ks[15], (DEPTH, LRU_WIDTH, D_MODEL), LRU_WIDTH),
        "w_out": nrm(ks[16], (DEPTH, D_MODEL, D_MODEL), D_MODEL),
        "ffn_w_up": nrm(ks[17], (DEPTH, D_MODEL, 2 * D_FF), D_MODEL),
        "ffn_conv_w": nrm(ks[18], (DEPTH, FFN_CONV_K, 2 * D_FF), FFN_CONV_K),
        "ffn_conv_b": 0.02 * jax.random.normal(ks[19], (DEPTH, 2 * D_FF), f32),
        "ffn_w_down": nrm(ks[20], (DEPTH, D_FF, D_MODEL), D_FF),
    }


def reference(x, norm_mix_pre, norm_mix_post, norm_ffn_pre, norm_ffn_post, w_in, conv_short_w,
              w_conv_branch, lru_conv_w, lru_conv_b, lru_wa, lru_ba, lru_wx, lru_bx, lru_lambda,
              w_lru_branch, w_out, ffn_w_up, ffn_conv_w, ffn_conv_b, ffn_w_down):
    for l in range(DEPTH):
        h = rmsnorm(x, norm_mix_pre[l])
        proj = jnp.einsum("bsd,dc->bsc", h, w_in[l])
        c_b, c_c, c_x, l_x, l_y, g_conv, g_lru = jnp.split(proj, SPLITS, axis=-1)
        y_a = c_b * causal_dwconv(c_c * c_x, conv_short_w[l])
        xl = causal_dwconv(l_x, lru_conv_w[l], lru_conv_b[l])
        hl = rg_lru(xl, lru_wa[l], lru_ba[l], lru_wx[l], lru_bx[l], lru_lambda[l])
        y_b = hl * jax.nn.gelu(l_y, approximate=True)
        merged = (jax.nn.sigmoid(g_conv) * jnp.einsum("bsc,cd->bsd", y_a, w_conv_branch[l])
                  + jax.nn.sigmoid(g_lru) * jnp.einsum("bsc,cd->bsd", y_b, w_lru_branch[l]))
        mix = jnp.einsum("bsd,de->bse", merged, w_out[l])
        x = x + rmsnorm(mix, norm_mix_post[l])
        h = rmsnorm(x, norm_ffn_pre[l])
        up = jnp.einsum("bsd,df->bsf", h, ffn_w_up[l])
        up = causal_dwconv(up, ffn_conv_w[l], ffn_conv_b[l])
        gate, val = jnp.split(up, 2, axis=-1)
        f = jax.nn.gelu(gate, approximate=True) * val
        out = jnp.einsum("bsf,fd->bsd", f, ffn_w_down[l])
        x = x + rmsnorm(out, norm_ffn_post[l])
    return x
```

```python
import numpy as np
import concourse.bass as bass
import concourse.mybir as mybir
from concourse.bass_utils import run_bass_kernel_spmd

F32 = mybir.dt.float32
F32R = mybir.dt.float32r
BF16 = mybir.dt.bfloat16
AF = mybir.ActivationFunctionType
ALU = mybir.AluOpType

D = 1024
S = 2048
NCH = 8
NFF = 24
NCORES = 8
EPS = 1e-6
QW = 512
HW_ = 1024

C_G1, C_G2, C_G3, C_G4 = 0, 8, 16, 24
C_CSW = 32
C_LCW = 56
C_LCB = 88
C_LBA = 96
C_LBX = 104
C_LAM = 112
C_FCW = 120
C_FCB = 264
C_LS8 = 312
C_LS4 = 320
C_HBA = 328
C_HBX = 336
C_TMP = 344
NV = 360
assert NV * 4 <= 1536
YB_ENG = "dve"

RX = 0
RH = 65536
RY = 98304
RE = 196608
E_CV = RE
E_ONES = E_CV + 1536
E_RSTD2 = E_ONES + 256
E_HALO = E_RSTD2 + 8192
E_SQ = E_HALO + 512
E_END = E_SQ + 4096
ARENA_BYTES = E_END
BLK = 256


class View:
    __slots__ = ("space", "ap", "ranges", "esz", "lo")

    def __init__(self, space, ap, ranges, esz, lo):
        self.space, self.ap, self.ranges, self.esz, self.lo = space, ap, ranges, esz, lo

    def sl(self, a, b):
        lo = self.lo + a * self.esz
        return View(self.space, self.ap[:, a:b], [(lo, self.lo + b * self.esz)], self.esz, lo)


class Op:
    __slots__ = ("eng", "emit", "reads", "writes", "count", "waits", "chan", "dma_val", "is_dma", "name", "dbg")


class Builder:
    def __init__(self, nc, arena, psum, rarena):
        self.nc = nc
        self.arena = arena
        self.psum = psum
        self.rarena = rarena
        self.ops = {e: [] for e in ("pe", "act", "dve", "pool", "sp")}
        self.ncompute = {e: 0 for e in ("pe", "act", "dve", "pool", "sp")}
        self.lastw = {"sb": {}, "ps": {}, "sr": {}, "tk": {}}
        self.readers = {"sb": {}, "ps": {}, "sr": {}, "tk": {}}
        self.seen = {e: {} for e in ("pe", "act", "dve", "pool", "sp")}
        self.chan_val = {}
        self.out_chans = []

    def sb(self, off, dtype, n):
        esz = 2 if dtype == BF16 else 4
        assert off % 4 == 0 and (n * esz) % 4 == 0
        a = self.arena[:, off // 4:(off + n * esz) // 4]
        if dtype != F32:
            a = a.bitcast(dtype)
        return View("sb", a, [(off, off + n * esz)], esz, off)

    def sb3(self, off, dtype, k, n, c0, c1, kstride=None):
        esz = 2 if dtype == BF16 else 4
        a = self.arena[:, off // 4:(off + k * n * esz) // 4]
        if dtype != F32:
            a = a.bitcast(dtype)
        a = a.rearrange("p (k n) -> p k n", k=k)[:, :, c0:c1]
        rg = [(off + (i * n + c0) * esz, off + (i * n + c1) * esz) for i in range(k)]
        return View("sb", a, rg, esz, off)

    def sr(self, e0, n):
        return View("sr", self.rarena[:, e0:e0 + n], [(e0 * 4, (e0 + n) * 4)], 4, e0 * 4)

    def tok(self, i):
        return View("tk", None, [(i * BLK, (i + 1) * BLK)], 4, i * BLK)

    def ps(self, col0, n):
        return View("ps", self.psum[:, col0:col0 + n], [(col0 * 4, (col0 + n) * 4)], 4, col0 * 4)

    @staticmethod
    def _blocks(v):
        for lo, hi in v.ranges:
            for b in range(lo // BLK, (hi - 1) // BLK + 1):
                yield b

    def add(self, eng, emit, reads=(), writes=(), chan=None, name=""):
        op = Op()
        op.eng, op.emit, op.name = eng, emit, name
        op.is_dma = chan is not None
        op.chan = chan
        deps = {}
        for v in reads:
            lw = self.lastw[v.space]
            for b in self._blocks(v):
                w = lw.get(b)
                if w is not None:
                    deps[id(w)] = w
        for v in writes:
            lw = self.lastw[v.space]
            rd = self.readers[v.space]
            for b in self._blocks(v):
                w = lw.get(b)
                if w is not None:
                    deps[id(w)] = w
                r = rd.get(b)
                if r:
                    for o in r.values():
                        deps[id(o)] = o
        for v in reads:
            rd = self.readers[v.space]
            for b in self._blocks(v):
                r = rd.get(b)
                if r is None:
                    r = rd[b] = {}
                key = eng if not op.is_dma else ("dma", id(op))
                r[key] = op
        for v in writes:
            lw = self.lastw[v.space]
            rd = self.readers[v.space]
            for b in self._blocks(v):
                lw[b] = op
                rd[b] = {}
        if op.is_dma:
            self.chan_val[chan] = self.chan_val.get(chan, 0) + 16
            op.dma_val = self.chan_val[chan]
            op.count = None
        else:
            self.ncompute[eng] += 1
            op.count = self.ncompute[eng]
        waits = []
        seen = self.seen[eng]
        for d in deps.values():
            if d is op:
                continue
            if d.is_dma:
                key, val = ("chan", d.chan), d.dma_val
            else:
                if d.eng == "pe" and eng == "pe" and not op.is_dma:
                    continue
                key, val = ("eng", d.eng), d.count
            if seen.get(key, 0) < val:
                seen[key] = val
                waits.append((key, val))
        best = {}
        for key, val in waits:
            if best.get(key, 0) < val:
                best[key] = val
        op.waits = list(best.items())
        op.dbg = [(d.name, d.eng, d.count if not d.is_dma else d.dma_val) for d in deps.values()]
        self.ops[eng].append(op)
        return op


def build_program(debug=None):
    nc = bass.Bass("TRN2", target_bir_lowering=False)
    xT = nc.dram_tensor("xT", [D, S], F32, kind="ExternalInput").ap()
    cvec_d = nc.dram_tensor("cvec", [128, NV], F32, kind="ExternalInput").ap()
    wA_d = nc.dram_tensor("wA", [8, 128, 8 * 384], F32, kind="ExternalInput").ap()
    wB_d = nc.dram_tensor("wB", [4, 128, 8 * 512], F32, kind="ExternalInput").ap()
    wG_d = nc.dram_tensor("wG", [4, 128, 2 * 512], F32, kind="ExternalInput").ap()
    wM_d = nc.dram_tensor("wM", [8, 128, 8 * 512], F32, kind="ExternalInput").ap()
    wO_d = nc.dram_tensor("wO", [128, 8 * 1024], F32, kind="ExternalInput").ap()
    wU_d = nc.dram_tensor("wU", [24, 128, 8 * 256], F32, kind="ExternalInput").ap()
    wD_d = nc.dram_tensor("wD", [8, 128, 24 * 128], F32, kind="ExternalInput").ap()
    yT = nc.dram_tensor("yT", [D, S], F32, kind="ExternalOutput").ap()
    dbg_d = None
    if debug:
        dbg_d = nc.dram_tensor("dbg", [128, 8 * 2048], F32, kind="ExternalOutput").ap()

    from contextlib import ExitStack
    with ExitStack() as es:
        arena = es.enter_context(nc.sbuf_tensor("arena", [128, ARENA_BYTES // 4], F32))
        psum = es.enter_context(nc.psum_tensor("psum", [128, 4096], F32))
        B = Builder(nc, arena, psum, None)
        sem_eng = {e: es.enter_context(nc.semaphore("sem_" + e)) for e in ("pe", "act", "dve", "pool")}
        chan_sems = {}

        def chan_sem(c):
            if c not in chan_sems:
                chan_sems[c] = es.enter_context(nc.semaphore("ch_" + c))
            return chan_sems[c]

        def dma(queue, out_view, in_ap, chan, name="", out_is_dram=False, reads=(), writes=()):
            chan_sem(chan)
            if out_is_dram:
                return B.add(queue, lambda e, o=out_view, i=in_ap: e.dma_start(out=o, in_=i.ap),
                             reads=[in_ap], writes=[], chan=chan, name=name)
            return B.add(queue, lambda e, o=out_view, i=in_ap: e.dma_start(out=o.ap, in_=i),
                         reads=list(reads), writes=[out_view] + list(writes), chan=chan, name=name)

        def act(out, in_, func, bias=None, scale=1.0, name="", extra_reads=()):
            kw = {}
            rd = [in_] + list(extra_reads)
            if bias is not None:
                if isinstance(bias, View):
                    kw["bias"] = bias.ap
                    rd.append(bias)
                else:
                    kw["bias"] = bias
            if isinstance(scale, View):
                rd.append(scale)
                sc = scale.ap
            else:
                sc = scale
            return B.add("act", lambda e, o=out, i=in_, f=func, sc=sc, kw=kw: e.activation(out=o.ap, in_=i.ap, func=f, scale=sc, **kw),
                         reads=rd, writes=[out], name=name)

        def tt(out, in0, in1, op, name="", eng="dve"):
            return B.add(eng, lambda e, o=out, a=in0, b=in1, op=op: e.tensor_tensor(out=o.ap, in0=a.ap, in1=b.ap, op=op),
                         reads=[in0, in1], writes=[out], name=name)

        def stt(out, in0, scalar, in1, op0, op1, name=""):
            rd = [in0, in1]
            if isinstance(scalar, View):
                rd.append(scalar)
                sc = scalar.ap
            else:
                sc = scalar
            return B.add("dve", lambda e, o=out, a=in0, sc=sc, b=in1, op0=op0, op1=op1: e.scalar_tensor_tensor(
                out=o.ap, in0=a.ap, scalar=sc, in1=b.ap, op0=op0, op1=op1), reads=rd, writes=[out], name=name)

        def ts(out, in0, s1, s2, op0, op1=None, name="", eng="dve"):
            rd = [in0]
            a1 = s1.ap if isinstance(s1, View) else s1
            a2 = s2.ap if isinstance(s2, View) else s2
            if isinstance(s1, View):
                rd.append(s1)
            if isinstance(s2, View):
                rd.append(s2)
            if op1 is None:
                return B.add(eng, lambda e, o=out, a=in0, a1=a1, op0=op0: e.tensor_scalar(
                    out=o.ap, in0=a.ap, scalar1=a1, scalar2=None, op0=op0), reads=rd, writes=[out], name=name)
            return B.add(eng, lambda e, o=out, a=in0, a1=a1, a2=a2, op0=op0, op1=op1: e.tensor_scalar(
                out=o.ap, in0=a.ap, scalar1=a1, scalar2=a2, op0=op0, op1=op1), reads=rd, writes=[out], name=name)

        def rstd_act(out, tmp, ss, name=""):
            act(tmp, ss, AF.Ln, bias=EPS, scale=1.0 / D, name=name + "_ln")
            act(out, tmp, AF.Exp, scale=-0.5, name=name)

        def recip(out, in_, name=""):
            return B.add("dve", lambda e, o=out, i=in_: e.reciprocal(out=o.ap, in_=i.ap), reads=[in_], writes=[out], name=name)

        def memset(out, val, eng="dve", name=""):
            return B.add(eng, lambda e, o=out, v=val: e.memset(o.ap, v), reads=[], writes=[out], name=name)

        def scan(out, d0, d1, init, name=""):
            rd = [d0, d1]
            if isinstance(init, View):
                rd.append(init)
                ini = init.ap
            else:
                ini = init
            return B.add("dve", lambda e, o=out, a=d0, b=d1, ini=ini: e.tensor_tensor_scan(
                out=o.ap, data0=a.ap, data1=b.ap, initial=ini, op0=ALU.mult, op1=ALU.add), reads=rd, writes=[out], name=name)

        def mm_group(mms, name=""):
            reads, writes = [], []
            for (o, l, r, st, sp) in mms:
                reads.append(l)
                reads.append(r)
                writes.append(o)

            def emit(e, mms=mms):
                inst = None
                for (o, l, r, st, sp) in mms:
                    inst = e.matmul(o.ap, l.ap, r.ap, start=st, stop=sp)
                return inst
            return B.add("pe", emit, reads=reads, writes=writes, name=name)

        def cv(col, n=1):
            return B.sb(E_CV + col * 4, F32, n)
        ones_r = B.sb(E_ONES, BF16, 128)
        sq_r = [B.sb(E_SQ + i * 1024, BF16, QW) for i in range(4)]
        rstd2 = B.sb(E_RSTD2, F32, S)

        def xv(k, c0=0, c1=S):
            return B.sb(RX + k * 8192, F32, S).sl(c0, c1)

        def hv(k, c0=0, c1=S):
            return B.sb(RH + k * 4096, BF16, S).sl(c0, c1)

        def yav(k, c0=0, c1=S):
            return B.sb(RY + k * 4096, BF16, S).sl(c0, c1)

        def ybv(k, c0=0, c1=S):
            return B.sb(RY + 32768 + k * 4096, BF16, S).sl(c0, c1)

        def mgv(k, c0=0, c1=S):
            return B.sb(RY + 65536 + k * 4096, BF16, S).sl(c0, c1)

        def wtile(off, kk, cols):
            full = B.sb(off, BF16, kk * cols)

            def get(k, c0, c1):
                return full.sl(k * cols + c0, k * cols + c1)
            return full, get

        dma("sp", cv(0, NV), cvec_d[:, :], "cv", name="cvec")
        memset(ones_r, 1.0, eng="dve", name="ones")
        act(cv(C_TMP, 8), cv(C_LAM, 8), AF.Exp, scale=-1.0, name="exp(-lam)")
        act(cv(C_TMP + 8, 8), cv(C_TMP, 8), AF.Ln, bias=1.0, name="ln1p")
        ts(cv(C_LS8, 8), cv(C_TMP + 8, 8), -8.0, None, ALU.mult, name="ls8")
        ts(cv(C_LS4, 8), cv(C_TMP + 8, 8), -4.0, None, ALU.mult, name="ls4")
        ts(cv(C_HBA, 16), cv(C_LBA, 16), 0.5, None, ALU.mult, name="half_biases")

        xT3 = xT.rearrange("(k p) t -> p k t", p=128)
        for q in range(4):
            dma("sp", B.sb3(RX, F32, 8, S, q * QW, (q + 1) * QW), xT3[:, :, q * QW:(q + 1) * QW], "x%d" % q, name="xload%d" % q,
                writes=[B.tok(q)])

        def norm_scratch(base, nsq=4, nsd=2):
            sq = sqr = sq_r
            o = base
            sd = [B.sb(o + i * 2048, F32, QW) for i in range(nsd)]
            o += nsd * 2048
            rs = [B.sb(o + i * 2048, F32, QW) for i in range(nsd)]
            return sq, sqr, sd, rs

        sq0, sq0r, sd0, rs0 = norm_scratch(RY)
        nsq = [0]

        def sumsq_chunk(src_view, ss_ps, first, last, sqs, sqrs):
            i = nsq[0] % len(sqs)
            nsq[0] += 1
            act(sqrs[i], src_view, AF.Square, name="sq")
            mm_group([(ss_ps, ones_r, sqrs[i], first, last)], name="ssmm")

        for q in range(4):
            ss = B.ps((4 + q) * 512, QW)
            for k in range(8):
                sumsq_chunk(xv(k, q * QW, (q + 1) * QW), ss, k == 0, k == 7, sq0, sq0r)
            rstd_act(rs0[q % 2], sd0[q % 2], ss, name="rstd1")
            for k in range(8):
                stt(hv(k, q * QW, (q + 1) * QW), xv(k, q * QW, (q + 1) * QW), cv(C_G1 + k), rs0[q % 2],
                    ALU.mult, ALU.mult, name="h")

        slot_i = [0]

        def next_slot():
            s = slot_i[0] % 2
            slot_i[0] += 1
            return s * 2048

        def full_tile(col0, wget, wc0, rhs, K, name=""):
            for hf in range(2):
                mms = []
                for n in (2 * hf, 2 * hf + 1):
                    for k in range(K):
                        mms.append((B.ps(col0 + n * 512, 512), wget(k, wc0, wc0 + 128), rhs(k, n * 512, (n + 1) * 512),
                                    k == 0, k == K - 1))
                mm_group(mms, name=name)

        class WStream:
            def __init__(self):
                self.tiles = []
                self.issued = 0
                self.slot_busy = {}
                self.index = {}
                self.gates = set()
                self.last_use = 0

            def declare(self, key, ring, slot, off, kk, cols, src, gate=None):
                self.index[key] = len(self.tiles)
                self.tiles.append(dict(key=key, off=off, kk=kk, cols=cols, src=src, chan="w%s%d" % (ring, slot),
                                       slotkey=(ring, slot), gate=gate))

            def open_gate(self, g):
                self.gates.add(g)
                self.pump(upto=self.last_use + LOOKAHEAD)

            def pump(self, upto=None):
                while self.issued < len(self.tiles):
                    t = self.tiles[self.issued]
                    if t["slotkey"] in self.slot_busy:
                        break
                    if t["gate"] is not None and t["gate"] not in self.gates:
                        break
                    if upto is not None and self.issued > upto:
                        break
                    full, get = wtile(t["off"], t["kk"], t["cols"])
                    dma("pool", full, t["src"], t["chan"], name="w_" + t["key"],
                        reads=([B.tok(0)] if t["key"] == "B0" else [B.tok(1)] if t["key"] == "B1" else []))
                    t["get"] = get
                    self.slot_busy[t["slotkey"]] = self.issued
                    self.issued += 1

            def use(self, key):
                n = self.index[key]
                self.last_use = max(self.last_use, n)
                if n >= self.issued:
                    self.pump(upto=n)
                assert n < self.issued, "weight tile %s cannot be issued (slot busy)" % key
                self.pump(upto=n + LOOKAHEAD)
                return self.tiles[n]["get"]

            def done(self, key):
                n = self.index[key]
                t = self.tiles[n]
                assert self.slot_busy.get(t["slotkey"]) == n
                del self.slot_busy[t["slotkey"]]
                self.pump(upto=n + LOOKAHEAD)

        LOOKAHEAD = 3
        WS = WStream()
        R1 = RY + 65536
        RG = RY + 65536 + 24576
        RM = RX + 40960
        RO_ = RY
        RU = RY + 49152
        RD = RY + 49152 + 16384
        r1n = [0]

        def decl_r1(key, kk, cols, src):
            WS.declare(key, "a", r1n[0] % 3, R1 + (r1n[0] % 3) * 8192, kk, cols, src)
            r1n[0] += 1
        decl_r1("B0", 8, 512, wB_d[0])
        decl_r1("B1", 8, 512, wB_d[1])
        WS.declare("G0", "g", 0, RG, 2, 512, wG_d[0])
        decl_r1("B2", 8, 512, wB_d[2])
        WS.declare("G1", "g", 1, RG + 2048, 2, 512, wG_d[1])
        decl_r1("B3", 8, 512, wB_d[3])
        WS.declare("G2", "g", 0, RG, 2, 512, wG_d[2])
        WS.declare("G3", "g", 1, RG + 2048, 2, 512, wG_d[3])
        for j in range(8):
            decl_r1("A%d" % j, 8, 384, wA_d[j])
        for j in range(8):
            WS.declare("M%d" % j, "m", j % 3, RM + (j % 3) * 8192, 8, 512, wM_d[j])
        WS.declare("O", "o", 0, RO_, 8, 1024, wO_d[:, :], gate="ya_dead")
        un = [0]
        dn = [0]
        for hf in range(2):
            for j in range(NFF):
                WS.declare("U%d_%d" % (hf, j), "u", un[0] % 4, RU + (un[0] % 4) * 4096, 8, 256, wU_d[j], gate="merge_done")
                un[0] += 1
            if hf == 0:
                for d in range(8):
                    WS.declare("D0_%d" % d, "d", d % 2, RD + (d % 2) * 6144, 24, 128, wD_d[d], gate="phaseO_done")
            else:
                d1slots = [("d", 0, RD), ("d", 1, RD + 6144), ("e", 0, RH), ("e", 1, RH + 6144),
                           ("f", 0, RU), ("f", 1, RU + 6144), ("d", 0, RD), ("d", 1, RD + 6144)]
                for d in range(8):
                    ring, slot, off = d1slots[d]
                    WS.declare("D1_%d" % d, ring, slot, off, 24, 128, wD_d[d], gate=(None if d < 2 else "up1_done"))

        XL = RX
        XLB0 = RX + 32768
        XLB1 = RY + 24576
        SET0 = RX + 40960
        SET1 = RY
        HALO3 = E_HALO
        bslot = [0]

        def next_bslot():
            s_ = bslot[0] % 4
            bslot[0] += 1
            return s_ * 1024

        def half_tile(col0, wget, wc0, rhs, K, hf, name=""):
            for n in range(2):
                mms = []
                for k in range(K):
                    mms.append((B.ps(col0 + n * 512, 512), wget(k, wc0, wc0 + 128),
                                rhs(k, hf * HW_ + n * 512, hf * HW_ + (n + 1) * 512), k == 0, k == K - 1))
                mm_group(mms, name=name)

        def xlv(j):
            return B.sb(XL + (j % 4) * 8192, F32, S)

        def xlbv(j):
            return B.sb((XLB0 if (j // 2) % 2 == 0 else XLB1) + (j % 2) * 4096, BF16, S)

        def bset(j):
            base = SET0 if j % 2 == 0 else SET1
            return dict(A=[B.sb(base + hf * 12288, F32, HW_) for hf in range(2)],
                        Bm=[B.sb(base + hf * 12288 + 4096, F32, HW_) for hf in range(2)],
                        C=[B.sb(base + hf * 12288 + 8192, F32, HW_) for hf in range(2)])
        lxP = {}

        def lx_pe(j):
            wget = WS.use("B%d" % (j // 2))
            c = j % 2
            assert bslot[0] % 2 == 0
            for hf in range(2):
                col = next_bslot()
                half_tile(col, wget, c * 128, hv, 8, hf, name="lx%d" % j)
                if hf == 0:
                    lxP[j] = col

        def lx_act(j, hf):
            xl = xlv(j)
            a, b = hf * HW_, (hf + 1) * HW_
            act(xl.sl(a, b), B.ps(lxP[j] + a, HW_), AF.Identity, bias=cv(C_LCB + j), scale=cv(C_LCW + 3 * 8 + j), name="lconv_t3")

        def lx_dve(j):
            xl = xlv(j)
            xb = xlbv(j)
            P = B.ps(lxP[j], S)
            for hf in range(2):
                a, b = hf * HW_, (hf + 1) * HW_
                for tap, sh in ((2, 1), (1, 2), (0, 3)):
                    lo = max(a, sh)
                    stt(xl.sl(lo, b), P.sl(lo - sh, b - sh), cv(C_LCW + tap * 8 + j), xl.sl(lo, b), ALU.mult, ALU.add,
                        name="lconv_t%d" % tap)
            chan_sem("xc%d" % (j % 4))
            B.add("pool", lambda e, o=xb, i=xl: e.dma_start(out=o.ap, in_=i.ap), reads=[xl], writes=[xb],
                  chan="xc%d" % (j % 4), name="xl_bf_dma")

        def gate_pe_tanh(j):
            hd, c = j // 2, j % 2
            gget = WS.use("G%d" % hd)

            def xlb_rhs(k, a, b):
                return xlbv(2 * hd + k).sl(a, b)
            st = bset(j)
            for hf in range(2):
                ca = next_bslot()
                half_tile(ca, gget, c * 128, xlb_rhs, 2, hf, name="za%d" % j)
                act(st["A"][hf], B.ps(ca, HW_), AF.Tanh, bias=cv(C_HBA + j), scale=0.5, name="r'")
                cx_ = next_bslot()
                half_tile(cx_, gget, 256 + c * 128, xlb_rhs, 2, hf, name="zx%d" % j)
                act(st["C"][hf], B.ps(cx_, HW_), AF.Tanh, bias=cv(C_HBX + j), scale=0.5, name="i'")
            if c == 1:
                WS.done("G%d" % hd)

        def gate_exp_a2(j):
            st = bset(j)
            for hf in range(2):
                act(st["Bm"][hf], st["A"][hf], AF.Exp, bias=cv(C_LS8 + j), scale=cv(C_LS8 + j), name="a2")

        def gate_exp_a(j):
            st = bset(j)
            for hf in range(2):
                act(st["A"][hf], st["A"][hf], AF.Exp, bias=cv(C_LS4 + j), scale=cv(C_LS4 + j), name="a")

        def gate_sqrt(j):
            st = bset(j)
            for hf in range(2):
                act(st["Bm"][hf], st["Bm"][hf], AF.Sqrt, bias=0.25, scale=-0.25, name="mult/2")

        def gate_dve(j):
            st = bset(j)
            xl = xlv(j)
            memset(st["Bm"][0].sl(0, 1), 0.5, eng="dve", name="mult0")
            for hf in range(2):
                a, b = hf * HW_, (hf + 1) * HW_
                stt(st["C"][hf], st["C"][hf], 1.0, xl.sl(a, b), ALU.add, ALU.mult, name="(i'+1)*x")
            for hf in range(2):
                tt(st["C"][hf], st["C"][hf], st["Bm"][hf], ALU.mult, name="u")
            scan(st["Bm"][0], st["A"][0], st["C"][0], 0.0, name="scan0")
            scan(st["Bm"][1], st["A"][1], st["C"][1], st["Bm"][0].sl(HW_ - 1, HW_), name="scan1")

        def ly_pe_act(j):
            hd, c = j // 2, j % 2
            wget = WS.use("B%d" % hd)
            st = bset(j)
            for hf in range(2):
                cy = next_bslot()
                half_tile(cy, wget, 256 + c * 128, hv, 8, hf, name="ly%d" % j)
                act(st["C"][hf], B.ps(cy, HW_), AF.Gelu_apprx_tanh, name="gelu_ly")
            if c == 1:
                WS.done("B%d" % hd)

        def yb_dve(j):
            st = bset(j)
            for hf in range(2):
                a, b = hf * HW_, (hf + 1) * HW_
                tt(ybv(j, a, b), st["Bm"][hf], st["C"][hf], ALU.mult, name="yb", eng=YB_ENG)

        for j in range(-2, 9):
            cur = 0 <= j < 8
            nxt = 0 <= j + 2 < 8
            prv = 0 <= j - 1 < 8
            if cur:
                gate_pe_tanh(j)
            if nxt:
                lx_pe(j + 2)
            if prv:
                gate_dve(j - 1)
            if cur:
                gate_exp_a2(j)
            if nxt:
                lx_act(j + 2, 0)
            if cur:
                gate_exp_a(j)
            if nxt:
                lx_act(j + 2, 1)
            if cur:
                gate_sqrt(j)
            if prv:
                ly_pe_act(j - 1)
            if nxt:
                lx_dve(j + 2)
            if prv:
                yb_dve(j - 1)

        SC_A = RX
        for j in range(8):
            wget = WS.use("A%d" % j)
            sb_ = SC_A + (j % 2) * 16384
            cxq = B.sb(sb_, F32, S)
            pb = B.sb(sb_ + 8192, F32, S)
            c0 = next_slot()
            full_tile(c0, wget, 0, hv, 8, name="cx%d" % j)
            for hf in range(2):
                a, b = hf * HW_, (hf + 1) * HW_
                act(cxq.sl(a, b), B.ps(c0 + a, HW_), AF.Copy, name="cxcopy")
            c1 = next_slot()
            full_tile(c1, wget, 128, hv, 8, name="cc%d" % j)
            for hf in range(2):
                a, b = hf * HW_, (hf + 1) * HW_
                tt(pb.sl(a, b), B.ps(c1 + a, HW_), cxq.sl(a, b), ALU.mult, name="p")
            act(cxq, pb, AF.Identity, scale=cv(C_CSW + 2 * 8 + j), name="conv_t2")
            stt(cxq.sl(1, S), pb.sl(0, S - 1), cv(C_CSW + 1 * 8 + j), cxq.sl(1, S), ALU.mult, ALU.add, name="conv_t1")
            stt(cxq.sl(2, S), pb.sl(0, S - 2), cv(C_CSW + 0 * 8 + j), cxq.sl(2, S), ALU.mult, ALU.add, name="conv_t0")
            c2 = next_slot()
            full_tile(c2, wget, 256, hv, 8, name="cb%d" % j)
            for hf in range(2):
                a, b = hf * HW_, (hf + 1) * HW_
                tt(yav(j, a, b), B.ps(c2 + a, HW_), cxq.sl(a, b), ALU.mult, name="ya")
            WS.done("A%d" % j)

        SC_M = RX
        for j in range(8):
            wget = WS.use("M%d" % j)
            cgc = next_slot()
            full_tile(cgc, wget, 0, hv, 8, name="gc%d" % j)
            S1 = [B.sb(SC_M + (j % 2) * 16384 + hf * 4096, F32, HW_) for hf in range(2)]
            S2 = [B.sb(SC_M + (j % 2) * 16384 + 8192 + hf * 4096, F32, HW_) for hf in range(2)]
            for hf in range(2):
                act(S1[hf], B.ps(cgc + hf * HW_, HW_), AF.Sigmoid, name="sg_conv")
            cba = next_slot()
            full_tile(cba, wget, 128, yav, 8, name="brA%d" % j)
            if j == 7:
                WS.open_gate("ya_dead")
            for hf in range(2):
                tt(S1[hf], B.ps(cba + hf * HW_, HW_), S1[hf], ALU.mult, name="mA")
            cgl = next_slot()
            full_tile(cgl, wget, 256, hv, 8, name="gl%d" % j)
            for hf in range(2):
                act(S2[hf], B.ps(cgl + hf * HW_, HW_), AF.Sigmoid, name="sg_lru")
            cbb = next_slot()
            full_tile(cbb, wget, 384, ybv, 8, name="brB%d" % j)
            for hf in range(2):
                a, b = hf * HW_, (hf + 1) * HW_
                tt(S2[hf], B.ps(cbb + a, HW_), S2[hf], ALU.mult, name="mB")
                tt(mgv(j, a, b), S1[hf], S2[hf], ALU.add, name="merged")
            WS.done("M%d" % j)

        WS.open_gate("merge_done")
        woget = WS.use("O")
        MIXB = RY + 16384
        _, _, sdO, rsO = norm_scratch(RH + 16384)
        for q in range(4):
            dma("sp", B.sb3(RX, F32, 8, S, q * QW, (q + 1) * QW), xT3[:, :, q * QW:(q + 1) * QW], "x%d" % q, name="xreload%d" % q)
        bank_i = [0]
        EARLY_H2 = [None]

        def sq_act(src):
            i = nsq[0] % 4
            nsq[0] += 1
            act(sq_r[i], src, AF.Square, name="sq")
            return sq_r[i]

        def mixv(q, j):
            return B.sb(MIXB + (q % 2) * 16384 + j * 2048, F32, QW)

        ss2_pend = [None]

        def ss2_flush():
            if ss2_pend[0] is not None:
                q_, j_, sqv = ss2_pend[0]
                ss2 = B.ps(7 * 512, QW)
                mm_group([(ss2, ones_r, sqv, j_ == 0, j_ == 7)], name="ss2mm")
                if j_ == 7:
                    a_, b_ = q_ * QW, (q_ + 1) * QW
                    rstd_act(rstd2.sl(a_, b_), sdO[1], ss2, name="rstd2")
                ss2_pend[0] = None

        def ss2_piece(q, j):
            ss2_flush()
            a, b = q * QW, (q + 1) * QW
            ss2_pend[0] = (q, j, sq_act(xv(j, a, b)))

        def mix_stage(q):
            a, b = q * QW, (q + 1) * QW
            ss1 = B.ps(6 * 512, QW)
            pend = None
            for j in range(8):
                pcol = (bank_i[0] % 6) * 512
                bank_i[0] += 1
                P = B.ps(pcol, QW)
                mm_group([(P, woget(k, j * 128, (j + 1) * 128), mgv(k, a, b), k == 0, k == 7) for k in range(8)], name="mix%d" % j)
                if pend is not None:
                    mm_group([(ss1, ones_r, pend[0], pend[1] == 0, False)], name="ss1mm")
                ss2_flush()
                act(mixv(q, j), P, AF.Copy, name="mixcopy")
                pend = (sq_act(P), j)
                if q >= 1:
                    ss2_piece(q - 1, j)
            mm_group([(ss1, ones_r, pend[0], False, True)], name="ss1mm")
            ss2_flush()

        def norm_stage(q):
            a, b = q * QW, (q + 1) * QW
            ss1 = B.ps(6 * 512, QW)
            rstd_act(rsO[0], sdO[0], ss1, name="rstdO")
            for j in range(8):
                stt(mixv(q, j), mixv(q, j), cv(C_G2 + j), rsO[0], ALU.mult, ALU.mult, name="mixn")
                tt(xv(j, a, b), mixv(q, j), xv(j, a, b), ALU.add, name="x1")

        def _early_h2():
            for k in range(8):
                stt(B.sb(RH + k * 2048, BF16, HW_), xv(k, 0, HW_), cv(C_G3 + k), rstd2.sl(0, HW_), ALU.mult, ALU.mult, name="h2")
        EARLY_H2[0] = _early_h2
        for q in range(3):
            mix_stage(q)
            norm_stage(q)
        EARLY_H2[0]()
        mix_stage(3)
        WS.done("O")
        norm_stage(3)

        H2 = RH
        OSB = RH + 16384
        FB = RY
        GV = RY + 49152 + 16384 + 12288
        FSC = GV + 16384
        _, _, sdF, rsF = norm_scratch(FSC, 2, 1)
        assert FSC + 4096 <= RE
        HALO = E_HALO
        bank_u = [0]
        bank_d = [0]
        uidx = [0]
        didx = [0]
        yT3 = yT.rearrange("(k p) t -> p k t", p=128)

        def h2_stage(hf):
            a, b = hf * HW_, (hf + 1) * HW_
            for k in range(8):
                stt(B.sb(H2 + k * 2048, BF16, HW_), xv(k, a, b), cv(C_G3 + k), rstd2.sl(a, b), ALU.mult, ALU.mult, name="h2")

        def h2v(k, c0, c1):
            return B.sb(H2 + k * 2048, BF16, HW_).sl(c0, c1)

        def gvbuf(j, gi):
            return B.sb(GV + ((j % 2) * 2 + gi) * 4096, F32, HW_)

        def up_stage(hf, j):
            wget = WS.use("U%d_%d" % (hf, j))
            Ps = []
            for gi in range(2):
                pcol = ((bank_u[0] % 3) if bank_u[0] < 24 else (bank_u[0] % 4)) * 1024
                bank_u[0] += 1
                mms = []
                for n in range(2):
                    for k in range(8):
                        mms.append((B.ps(pcol + n * 512, 512), wget(k, gi * 128, gi * 128 + 128), h2v(k, n * 512, (n + 1) * 512),
                                    k == 0, k == 7))
                mm_group(mms, name="up%d_%d" % (j, gi))
                Ps.append(B.ps(pcol, HW_))
            WS.done("U%d_%d" % (hf, j))
            for gi in range(2):
                ch = j if gi == 0 else NFF + j
                P, t = Ps[gi], gvbuf(j, gi)
                halo = B.sb(HALO + ch * 8, F32, 2)
                w1, w0 = cv(C_FCW + 1 * 48 + ch), cv(C_FCW + 0 * 48 + ch)
                act(t, P, AF.Identity, bias=cv(C_FCB + ch), scale=cv(C_FCW + 2 * 48 + ch), name="fconv_t2")
                if hf == 0:
                    act(halo, P.sl(HW_ - 2, HW_), AF.Copy, name="halo")
                else:
                    act(t.sl(0, 1), halo.sl(1, 2), AF.Identity, bias=t.sl(0, 1), scale=w1, name="fconv_b1")
                    act(t.sl(0, 1), halo.sl(0, 1), AF.Identity, bias=t.sl(0, 1), scale=w0, name="fconv_b0a")
                    act(t.sl(1, 2), halo.sl(1, 2), AF.Identity, bias=t.sl(1, 2), scale=w0, name="fconv_b0b")
            for gi in range(2):
                ch = j if gi == 0 else NFF + j
                P, t = Ps[gi], gvbuf(j, gi)
                stt(t.sl(1, HW_), P.sl(0, HW_ - 1), cv(C_FCW + 1 * 48 + ch), t.sl(1, HW_), ALU.mult, ALU.add, name="fconv_t1")
                stt(t.sl(2, HW_), P.sl(0, HW_ - 2), cv(C_FCW + 0 * 48 + ch), t.sl(2, HW_), ALU.mult, ALU.add, name="fconv_t0")
            g, v = gvbuf(j, 0), gvbuf(j, 1)
            act(g, g, AF.Gelu_apprx_tanh, name="gelu_f")
            tt(B.sb(FB + j * 2048, BF16, HW_), g, v, ALU.mult, name="f")

        def fv(k, c0, c1):
            return B.sb(FB + k * 2048, BF16, HW_).sl(c0, c1)

        def odv(qq, d):
            return B.sb((OSB if qq == 0 else GV) + d * 2048, F32, QW)

        def down_stage(hf, order, mid_hook=None):
            ss3 = [B.ps(6 * 512, QW), B.ps(7 * 512, QW)]
            pend = None
            KS = 20
            seen_d = {}
            cnt = [0, 0]
            for (d, qq) in order:
                wget = WS.use("D%d_%d" % (hf, d))
                pcol = (bank_d[0] % 6) * 512
                bank_d[0] += 1
                P = B.ps(pcol, QW)
                mm_group([(P, wget(k, 0, 128), fv(k, qq * QW, (qq + 1) * QW), k == 0, False) for k in range(KS)],
                         name="down%d_%d a" % (d, qq))
                mm_group([(P, wget(k, 0, 128), fv(k, qq * QW, (qq + 1) * QW), False, k == NFF - 1) for k in range(KS, NFF)],
                         name="down%d_%d b" % (d, qq))
                if pend is not None:
                    mm_group([(ss3[pend[2]], ones_r, pend[0], pend[3] == 0, pend[3] == 7)], name="ss3mm")
                    if pend[3] == 7 and mid_hook is not None and pend[2] == 0:
                        mid_hook()
                act(odv(qq, d), P, AF.Copy, name="ocopy")
                pend = (sq_act(P), d, qq, cnt[qq])
                cnt[qq] += 1
                seen_d[d] = seen_d.get(d, 0) + 1
                if seen_d[d] == 2:
                    WS.done("D%d_%d" % (hf, d))
            mm_group([(ss3[pend[2]], ones_r, pend[0], pend[3] == 0, pend[3] == 7)], name="ss3mm")

        def post_rstd(qq, rs, tmp):
            ss3 = B.ps((6 + qq) * 512, QW)
            rstd_act(rs, tmp, ss3, name="rstdF")

        def post_apply(hf, qq, rs):
            q = 2 * hf + qq
            ga, gb = q * QW, (q + 1) * QW
            nsplit = 4 if q == 3 else 1
            per = 8 // nsplit
            for d in range(8):
                stt(odv(qq, d), odv(qq, d), cv(C_G4 + d), rs, ALU.mult, ALU.mult, name="on")
                tt(xv(d, ga, gb), odv(qq, d), xv(d, ga, gb), ALU.add, name="out")
                if (d + 1) % per == 0:
                    k0 = d + 1 - per
                    dma("sp", yT3[:, k0:d + 1, ga:gb], B.sb3(RX + k0 * 8192, F32, per, S, ga, gb), "out%d" % q,
                        name="store%d_%d" % (q, k0), out_is_dram=True)
            B.out_chans.append(("out%d" % q, B.chan_val["out%d" % q]))

        ORDER0 = [(d, qq) for d in range(8) for qq in range(2)]
        ORDER1 = [(0, 0), (1, 0), (2, 0), (3, 0), (4, 0), (0, 1), (5, 0), (1, 1), (6, 0), (2, 1), (7, 0), (3, 1),
                  (4, 1), (5, 1), (6, 1), (7, 1)]
        rs_def = B.sb(E_RSTD2, F32, QW)
        rs_tmp = B.sb(E_RSTD2 + 2048, F32, QW)
        for hf in range(2):
            for j in range(NFF):
                up_stage(hf, j)
                if hf == 0 and 2 <= j < 10:
                    ss2_piece(3, j - 2)
                if hf == 0 and j == 10:
                    ss2_flush()
                    WS.open_gate("phaseO_done")
            if hf == 0:
                h2_stage(1)
                down_stage(0, ORDER0)
                post_rstd(1, rsF[0], sdF[0])
                post_rstd(0, rs_def, rs_tmp)
                post_apply(0, 1, rsF[0])
            else:
                WS.open_gate("up1_done")
                post_apply(0, 0, rs_def)

                def _mid():
                    post_rstd(0, rsF[0], sdF[0])
                    post_apply(1, 0, rsF[0])
                down_stage(1, ORDER1, mid_hook=_mid)
                post_rstd(1, rsF[0], sdF[0])
                post_apply(1, 1, rsF[0])

        assert HALO + 48 * 8 <= ARENA_BYTES

        def sem_of(key):
            kind, nm = key
            return sem_eng[nm] if kind == "eng" else chan_sems[nm]

        def emit_engine(e, name):
            for op in B.ops[name]:
                for key, val in op.waits:
                    e.wait_ge(sem_of(key), val)
                inst = op.emit(e)
                if op.is_dma:
                    inst.then_inc(chan_sems[op.chan], 16)
                else:
                    inst.then_inc(sem_eng[name], 1)

        if debug == "deps":
            return B
        block = es.enter_context(nc.Block())

        @block.sync
        def _(e):
            emit_engine(e, "sp")
            for c, v in B.out_chans:
                e.wait_ge(chan_sems[c], v)

        @block.gpsimd
        def _(e):
            emit_engine(e, "pool")

        @block.tensor
        def _(e):
            emit_engine(e, "pe")

        @block.scalar
        def _(e):
            emit_engine(e, "act")

        @block.vector
        def _(e):
            emit_engine(e, "dve")
    return nc


def _tile_w(w):
    K, C = w.shape
    return np.ascontiguousarray(w.reshape(K // 128, 128, C).transpose(1, 0, 2))


def _chunkcols(v, n):
    return np.ascontiguousarray(np.asarray(v, np.float32).reshape(n, 128).T)


def prepare_inputs(x, norm_mix_pre, norm_mix_post, norm_ffn_pre, norm_ffn_post, w_in, conv_short_w,
                   w_conv_branch, lru_conv_w, lru_conv_b, lru_wa, lru_ba, lru_wx, lru_bx, lru_lambda,
                   w_lru_branch, w_out, ffn_w_up, ffn_conv_w, ffn_conv_b, ffn_w_down):
    f = np.float32
    w_in = np.asarray(w_in[0], f)
    cb, cc, cx = w_in[:, 0:1024], w_in[:, 1024:2048], w_in[:, 2048:3072]
    lx, ly = w_in[:, 3072:4096], w_in[:, 4096:5120]
    gconv, glru = w_in[:, 5120:6144], w_in[:, 6144:7168]
    wcb = np.asarray(w_conv_branch[0], f)
    wlb = np.asarray(w_lru_branch[0], f)

    def ch(m, j):
        return m[:, j * 128:(j + 1) * 128]
    wA = np.stack([_tile_w(np.concatenate([ch(cx, j), ch(cc, j), ch(cb, j)], axis=1)).reshape(128, -1) for j in range(8)])
    wB = np.stack([_tile_w(np.concatenate([ch(lx, 2 * h), ch(lx, 2 * h + 1), ch(ly, 2 * h), ch(ly, 2 * h + 1)], axis=1)).reshape(128, -1)
                   for h in range(4)])
    wa = np.asarray(lru_wa[0], f)
    wx = np.asarray(lru_wx[0], f)
    wG = np.stack([_tile_w(np.concatenate([wa[h], wx[h]], axis=1)).reshape(128, -1) for h in range(4)])
    wM = np.stack([_tile_w(np.concatenate([ch(gconv, j), ch(wcb, j), ch(glru, j), ch(wlb, j)], axis=1)).reshape(128, -1)
                   for j in range(8)])
    wO = _tile_w(np.asarray(w_out[0], f)).reshape(128, -1)
    wup = np.asarray(ffn_w_up[0], f)
    gate, val = wup[:, 0:3072], wup[:, 3072:6144]
    wU = np.stack([_tile_w(np.concatenate([ch(gate, j), ch(val, j)], axis=1)).reshape(128, -1) for j in range(24)])
    wdn = np.asarray(ffn_w_down[0], f)
    wD = np.stack([_tile_w(ch(wdn, d)).reshape(128, -1) for d in range(8)])

    cvec = np.zeros((128, NV), f)
    cvec[:, C_G1:C_G1 + 8] = _chunkcols(norm_mix_pre[0], 8)
    cvec[:, C_G2:C_G2 + 8] = _chunkcols(norm_mix_post[0], 8)
    cvec[:, C_G3:C_G3 + 8] = _chunkcols(norm_ffn_pre[0], 8)
    cvec[:, C_G4:C_G4 + 8] = _chunkcols(norm_ffn_post[0], 8)
    for t in range(3):
        cvec[:, C_CSW + t * 8:C_CSW + t * 8 + 8] = _chunkcols(conv_short_w[0][t], 8)
    for t in range(4):
        cvec[:, C_LCW + t * 8:C_LCW + t * 8 + 8] = _chunkcols(lru_conv_w[0][t], 8)
    cvec[:, C_LCB:C_LCB + 8] = _chunkcols(lru_conv_b[0], 8)
    cvec[:, C_LBA:C_LBA + 8] = _chunkcols(np.asarray(lru_ba[0]).reshape(-1), 8)
    cvec[:, C_LBX:C_LBX + 8] = _chunkcols(np.asarray(lru_bx[0]).reshape(-1), 8)
    cvec[:, C_LAM:C_LAM + 8] = _chunkcols(lru_lambda[0], 8)
    for t in range(3):
        cvec[:, C_FCW + t * 48:C_FCW + t * 48 + 48] = _chunkcols(ffn_conv_w[0][t], 48)
    cvec[:, C_FCB:C_FCB + 48] = _chunkcols(ffn_conv_b[0], 48)
    shared = dict(cvec=cvec, wA=wA, wB=wB, wG=wG, wM=wM, wO=wO, wU=wU, wD=wD)
    x = np.asarray(x, f)
    in_maps = []
    for b in range(NCORES):
        m = dict(shared)
        m["xT"] = np.ascontiguousarray(x[b].T)
        in_maps.append(m)
    return in_maps


def kernel(**inputs):
    in_maps = prepare_inputs(**inputs)
    nc = build_program()
    res = run_bass_kernel_spmd(nc, in_maps, core_ids=list(range(NCORES)))
    out = np.stack([np.ascontiguousarray(res.results[b]["yT"].T) for b in range(NCORES)], axis=0)
    return out.astype(np.float32)
```

```python
import numpy as np
import concourse.bass as bass
import concourse.mybir as mybir
from concourse.bass_utils import run_bass_kernel_spmd

F32 = mybir.dt.float32
F32R = mybir.dt.float32r
BF16 = mybir.dt.bfloat16
AF = mybir.ActivationFunctionType
ALU = mybir.AluOpType

D = 1024
S = 2048
NCH = 8
NFF = 24
NCORES = 8
EPS = 1e-6
QW = 512
HW_ = 1024

C_G1, C_G2, C_G3, C_G4 = 0, 8, 16, 24
C_CSW = 32
C_LCW = 56
C_LCB = 88
C_LBA = 96
C_LBX = 104
C_LAM = 112
C_FCW = 120
C_FCB = 264
C_LS8 = 312
C_LS4 = 320
C_HBA = 328
C_HBX = 336
C_TMP = 344
NV = 360
assert NV * 4 <= 1536
YB_ENG = "dve"

RX = 0
RH = 65536
RY = 98304
RE = 196608
E_CV = RE
E_ONES = E_CV + 1536
E_RSTD2 = E_ONES + 256
E_HALO = E_RSTD2 + 8192
E_SQ = E_HALO + 512
E_END = E_SQ + 4096
ARENA_BYTES = E_END
BLK = 256


class View:
    __slots__ = ("space", "ap", "ranges", "esz", "lo")

    def __init__(self, space, ap, ranges, esz, lo):
        self.space, self.ap, self.ranges, self.esz, self.lo = space, ap, ranges, esz, lo

    def sl(self, a, b):
        lo = self.lo + a * self.esz
        return View(self.space, self.ap[:, a:b], [(lo, self.lo + b * self.esz)], self.esz, lo)


class Op:
    __slots__ = ("eng", "emit", "reads", "writes", "count", "waits", "chan", "dma_val", "is_dma", "name", "dbg")


class Builder:
    def __init__(self, nc, arena, psum, rarena):
        self.nc = nc
        self.arena = arena
        self.psum = psum
        self.rarena = rarena
        self.ops = {e: [] for e in ("pe", "act", "dve", "pool", "sp")}
        self.ncompute = {e: 0 for e in ("pe", "act", "dve", "pool", "sp")}
        self.lastw = {"sb": {}, "ps": {}, "sr": {}, "tk": {}}
        self.readers = {"sb": {}, "ps": {}, "sr": {}, "tk": {}}
        self.seen = {e: {} for e in ("pe", "act", "dve", "pool", "sp")}
        self.chan_val = {}
        self.out_chans = []

    def sb(self, off, dtype, n):
        esz = 2 if dtype == BF16 else 4
        assert off % 4 == 0 and (n * esz) % 4 == 0
        a = self.arena[:, off // 4:(off + n * esz) // 4]
        if dtype != F32:
            a = a.bitcast(dtype)
        return View("sb", a, [(off, off + n * esz)], esz, off)

    def sb3(self, off, dtype, k, n, c0, c1, kstride=None):
        esz = 2 if dtype == BF16 else 4
        a = self.arena[:, off // 4:(off + k * n * esz) // 4]
        if dtype != F32:
            a = a.bitcast(dtype)
        a = a.rearrange("p (k n) -> p k n", k=k)[:, :, c0:c1]
        rg = [(off + (i * n + c0) * esz, off + (i * n + c1) * esz) for i in range(k)]
        return View("sb", a, rg, esz, off)

    def sr(self, e0, n):
        return View("sr", self.rarena[:, e0:e0 + n], [(e0 * 4, (e0 + n) * 4)], 4, e0 * 4)

    def tok(self, i):
        return View("tk", None, [(i * BLK, (i + 1) * BLK)], 4, i * BLK)

    def ps(self, col0, n):
        return View("ps", self.psum[:, col0:col0 + n], [(col0 * 4, (col0 + n) * 4)], 4, col0 * 4)

    @staticmethod
    def _blocks(v):
        for lo, hi in v.ranges:
            for b in range(lo // BLK, (hi - 1) // BLK + 1):
                yield b

    def add(self, eng, emit, reads=(), writes=(), chan=None, name=""):
        op = Op()
        op.eng, op.emit, op.name = eng, emit, name
        op.is_dma = chan is not None
        op.chan = chan
        deps = {}
        for v in reads:
            lw = self.lastw[v.space]
            for b in self._blocks(v):
                w = lw.get(b)
                if w is not None:
                    deps[id(w)] = w
        for v in writes:
            lw = self.lastw[v.space]
            rd = self.readers[v.space]
            for b in self._blocks(v):
                w = lw.get(b)
                if w is not None:
                    deps[id(w)] = w
                r = rd.get(b)
                if r:
                    for o in r.values():
                        deps[id(o)] = o
        for v in reads:
            rd = self.readers[v.space]
            for b in self._blocks(v):
                r = rd.get(b)
                if r is None:
                    r = rd[b] = {}
                key = eng if not op.is_dma else ("dma", id(op))
                r[key] = op
        for v in writes:
            lw = self.lastw[v.space]
            rd = self.readers[v.space]
            for b in self._blocks(v):
                lw[b] = op
                rd[b] = {}
        if op.is_dma:
            self.chan_val[chan] = self.chan_val.get(chan, 0) + 16
            op.dma_val = self.chan_val[chan]
            op.count = None
        else:
            self.ncompute[eng] += 1
            op.count = self.ncompute[eng]
        waits = []
        seen = self.seen[eng]
        for d in deps.values():
            if d is op:
                continue
            if d.is_dma:
                key, val = ("chan", d.chan), d.dma_val
            else:
                if d.eng == "pe" and eng == "pe" and not op.is_dma:
                    continue
                key, val = ("eng", d.eng), d.count
            if seen.get(key, 0) < val:
                seen[key] = val
                waits.append((key, val))
        best = {}
        for key, val in waits:
            if best.get(key, 0) < val:
                best[key] = val
        op.waits = list(best.items())
        op.dbg = [(d.name, d.eng, d.count if not d.is_dma else d.dma_val) for d in deps.values()]
        self.ops[eng].append(op)
        return op


def build_program(debug=None):
    nc = bass.Bass("TRN2", target_bir_lowering=False)
    xT = nc.dram_tensor("xT", [D, S], F32, kind="ExternalInput").ap()
    cvec_d = nc.dram_tensor("cvec", [128, NV], F32, kind="ExternalInput").ap()
    wA_d = nc.dram_tensor("wA", [8, 128, 8 * 384], F32, kind="ExternalInput").ap()
    wB_d = nc.dram_tensor("wB", [4, 128, 8 * 512], F32, kind="ExternalInput").ap()
    wG_d = nc.dram_tensor("wG", [4, 128, 2 * 512], F32, kind="ExternalInput").ap()
    wM_d = nc.dram_tensor("wM", [8, 128, 8 * 512], F32, kind="ExternalInput").ap()
    wO_d = nc.dram_tensor("wO", [128, 8 * 1024], F32, kind="ExternalInput").ap()
    wU_d = nc.dram_tensor("wU", [24, 128, 8 * 256], F32, kind="ExternalInput").ap()
    wD_d = nc.dram_tensor("wD", [8, 128, 24 * 128], F32, kind="ExternalInput").ap()
    yT = nc.dram_tensor("yT", [D, S], F32, kind="ExternalOutput").ap()
    dbg_d = None
    if debug:
        dbg_d = nc.dram_tensor("dbg", [128, 8 * 2048], F32, kind="ExternalOutput").ap()

    from contextlib import ExitStack
    with ExitStack() as es:
        arena = es.enter_context(nc.sbuf_tensor("arena", [128, ARENA_BYTES // 4], F32))
        psum = es.enter_context(nc.psum_tensor("psum", [128, 4096], F32))
        B = Builder(nc, arena, psum, None)
        sem_eng = {e: es.enter_context(nc.semaphore("sem_" + e)) for e in ("pe", "act", "dve", "pool")}
        chan_sems = {}

        def chan_sem(c):
            if c not in chan_sems:
                chan_sems[c] = es.enter_context(nc.semaphore("ch_" + c))
            return chan_sems[c]

        def dma(queue, out_view, in_ap, chan, name="", out_is_dram=False, reads=(), writes=()):
            chan_sem(chan)
            if out_is_dram:
                return B.add(queue, lambda e, o=out_view, i=in_ap: e.dma_start(out=o, in_=i.ap),
                             reads=[in_ap], writes=[], chan=chan, name=name)
            return B.add(queue, lambda e, o=out_view, i=in_ap: e.dma_start(out=o.ap, in_=i),
                         reads=list(reads), writes=[out_view] + list(writes), chan=chan, name=name)

        def act(out, in_, func, bias=None, scale=1.0, name="", extra_reads=()):
            kw = {}
            rd = [in_] + list(extra_reads)
            if bias is not None:
                if isinstance(bias, View):
                    kw["bias"] = bias.ap
                    rd.append(bias)
                else:
                    kw["bias"] = bias
            if isinstance(scale, View):
                rd.append(scale)
                sc = scale.ap
            else:
                sc = scale
            return B.add("act", lambda e, o=out, i=in_, f=func, sc=sc, kw=kw: e.activation(out=o.ap, in_=i.ap, func=f, scale=sc, **kw),
                         reads=rd, writes=[out], name=name)

        def tt(out, in0, in1, op, name="", eng="dve"):
            return B.add(eng, lambda e, o=out, a=in0, b=in1, op=op: e.tensor_tensor(out=o.ap, in0=a.ap, in1=b.ap, op=op),
                         reads=[in0, in1], writes=[out], name=name)

        def stt(out, in0, scalar, in1, op0, op1, name=""):
            rd = [in0, in1]
            if isinstance(scalar, View):
                rd.append(scalar)
                sc = scalar.ap
            else:
                sc = scalar
            return B.add("dve", lambda e, o=out, a=in0, sc=sc, b=in1, op0=op0, op1=op1: e.scalar_tensor_tensor(
                out=o.ap, in0=a.ap, scalar=sc, in1=b.ap, op0=op0, op1=op1), reads=rd, writes=[out], name=name)

        def ts(out, in0, s1, s2, op0, op1=None, name="", eng="dve"):
            rd = [in0]
            a1 = s1.ap if isinstance(s1, View) else s1
            a2 = s2.ap if isinstance(s2, View) else s2
            if isinstance(s1, View):
                rd.append(s1)
            if isinstance(s2, View):
                rd.append(s2)
            if op1 is None:
                return B.add(eng, lambda e, o=out, a=in0, a1=a1, op0=op0: e.tensor_scalar(
                    out=o.ap, in0=a.ap, scalar1=a1, scalar2=None, op0=op0), reads=rd, writes=[out], name=name)
            return B.add(eng, lambda e, o=out, a=in0, a1=a1, a2=a2, op0=op0, op1=op1: e.tensor_scalar(
                out=o.ap, in0=a.ap, scalar1=a1, scalar2=a2, op0=op0, op1=op1), reads=rd, writes=[out], name=name)

        def rstd_act(out, tmp, ss, name=""):
            act(tmp, ss, AF.Ln, bias=EPS, scale=1.0 / D, name=name + "_ln")
            act(out, tmp, AF.Exp, scale=-0.5, name=name)

        def recip(out, in_, name=""):
            return B.add("dve", lambda e, o=out, i=in_: e.reciprocal(out=o.ap, in_=i.ap), reads=[in_], writes=[out], name=name)

        def memset(out, val, eng="dve", name=""):
            return B.add(eng, lambda e, o=out, v=val: e.memset(o.ap, v), reads=[], writes=[out], name=name)

        def scan(out, d0, d1, init, name=""):
            rd = [d0, d1]
            if isinstance(init, View):
                rd.append(init)
                ini = init.ap
            else:
                ini = init
            return B.add("dve", lambda e, o=out, a=d0, b=d1, ini=ini: e.tensor_tensor_scan(
                out=o.ap, data0=a.ap, data1=b.ap, initial=ini, op0=ALU.mult, op1=ALU.add), reads=rd, writes=[out], name=name)

        def mm_group(mms, name=""):
            reads, writes = [], []
            for (o, l, r, st, sp) in mms:
                reads.append(l)
                reads.append(r)
                writes.append(o)

            def emit(e, mms=mms):
                inst = None
                for (o, l, r, st, sp) in mms:
                    inst = e.matmul(o.ap, l.ap, r.ap, start=st, stop=sp)
                return inst
            return B.add("pe", emit, reads=reads, writes=writes, name=name)

        def cv(col, n=1):
            return B.sb(E_CV + col * 4, F32, n)
        ones_r = B.sb(E_ONES, BF16, 128)
        sq_r = [B.sb(E_SQ + i * 1024, BF16, QW) for i in range(4)]
        rstd2 = B.sb(E_RSTD2, F32, S)

        def xv(k, c0=0, c1=S):
            return B.sb(RX + k * 8192, F32, S).sl(c0, c1)

        def hv(k, c0=0, c1=S):
            return B.sb(RH + k * 4096, BF16, S).sl(c0, c1)

        def yav(k, c0=0, c1=S):
            return B.sb(RY + k * 4096, BF16, S).sl(c0, c1)

        def ybv(k, c0=0, c1=S):
            return B.sb(RY + 32768 + k * 4096, BF16, S).sl(c0, c1)

        def mgv(k, c0=0, c1=S):
            return B.sb(RY + 65536 + k * 4096, BF16, S).sl(c0, c1)

        def wtile(off, kk, cols):
            full = B.sb(off, BF16, kk * cols)

            def get(k, c0, c1):
                return full.sl(k * cols + c0, k * cols + c1)
            return full, get

        dma("sp", cv(0, NV), cvec_d[:, :], "cv", name="cvec")
        memset(ones_r, 1.0, eng="dve", name="ones")
        act(cv(C_TMP, 8), cv(C_LAM, 8), AF.Exp, scale=-1.0, name="exp(-lam)")
        act(cv(C_TMP + 8, 8), cv(C_TMP, 8), AF.Ln, bias=1.0, name="ln1p")
        ts(cv(C_LS8, 8), cv(C_TMP + 8, 8), -8.0, None, ALU.mult, name="ls8")
        ts(cv(C_LS4, 8), cv(C_TMP + 8, 8), -4.0, None, ALU.mult, name="ls4")
        ts(cv(C_HBA, 16), cv(C_LBA, 16), 0.5, None, ALU.mult, name="half_biases")

        xT3 = xT.rearrange("(k p) t -> p k t", p=128)
        for q in range(4):
            dma("sp", B.sb3(RX, F32, 8, S, q * QW, (q + 1) * QW), xT3[:, :, q * QW:(q + 1) * QW], "x%d" % q, name="xload%d" % q,
                writes=[B.tok(q)])

        def norm_scratch(base, nsq=4, nsd=2):
            sq = sqr = sq_r
            o = base
            sd = [B.sb(o + i * 2048, F32, QW) for i in range(nsd)]
            o += nsd * 2048
            rs = [B.sb(o + i * 2048, F32, QW) for i in range(nsd)]
            return sq, sqr, sd, rs

        sq0, sq0r, sd0, rs0 = norm_scratch(RY)
        nsq = [0]

        def sumsq_chunk(src_view, ss_ps, first, last, sqs, sqrs):
            i = nsq[0] % len(sqs)
            nsq[0] += 1
            act(sqrs[i], src_view, AF.Square, name="sq")
            mm_group([(ss_ps, ones_r, sqrs[i], first, last)], name="ssmm")

        for q in range(4):
            ss = B.ps((4 + q) * 512, QW)
            for k in range(8):
                sumsq_chunk(xv(k, q * QW, (q + 1) * QW), ss, k == 0, k == 7, sq0, sq0r)
            rstd_act(rs0[q % 2], sd0[q % 2], ss, name="rstd1")
            for k in range(8):
                stt(hv(k, q * QW, (q + 1) * QW), xv(k, q * QW, (q + 1) * QW), cv(C_G1 + k), rs0[q % 2],
                    ALU.mult, ALU.mult, name="h")

        slot_i = [0]

        def next_slot():
            s = slot_i[0] % 2
            slot_i[0] += 1
            return s * 2048

        def full_tile(col0, wget, wc0, rhs, K, name=""):
            for hf in range(2):
                mms = []
                for n in (2 * hf, 2 * hf + 1):
                    for k in range(K):
                        mms.append((B.ps(col0 + n * 512, 512), wget(k, wc0, wc0 + 128), rhs(k, n * 512, (n + 1) * 512),
                                    k == 0, k == K - 1))
                mm_group(mms, name=name)

        class WStream:
            def __init__(self):
                self.tiles = []
                self.issued = 0
                self.slot_busy = {}
                self.index = {}
                self.gates = set()
                self.last_use = 0

            def declare(self, key, ring, slot, off, kk, cols, src, gate=None):
                self.index[key] = len(self.tiles)
                self.tiles.append(dict(key=key, off=off, kk=kk, cols=cols, src=src, chan="w%s%d" % (ring, slot),
                                       slotkey=(ring, slot), gate=gate))

            def open_gate(self, g):
                self.gates.add(g)
                self.pump(upto=self.last_use + LOOKAHEAD)

            def pump(self, upto=None):
                while self.issued < len(self.tiles):
                    t = self.tiles[self.issued]
                    if t["slotkey"] in self.slot_busy:
                        break
                    if t["gate"] is not None and t["gate"] not in self.gates:
                        break
                    if upto is not None and self.issued > upto:
                        break
                    full, get = wtile(t["off"], t["kk"], t["cols"])
                    dma("pool", full, t["src"], t["chan"], name="w_" + t["key"],
                        reads=([B.tok(0)] if t["key"] == "B0" else [B.tok(1)] if t["key"] == "B1" else []))
                    t["get"] = get
                    self.slot_busy[t["slotkey"]] = self.issued
                    self.issued += 1

            def use(self, key):
                n = self.index[key]
                self.last_use = max(self.last_use, n)
                if n >= self.issued:
                    self.pump(upto=n)
                assert n < self.issued, "weight tile %s cannot be issued (slot busy)" % key
                self.pump(upto=n + LOOKAHEAD)
                return self.tiles[n]["get"]

            def done(self, key):
                n = self.index[key]
                t = self.tiles[n]
                assert self.slot_busy.get(t["slotkey"]) == n
                del self.slot_busy[t["slotkey"]]
                self.pump(upto=n + LOOKAHEAD)

        LOOKAHEAD = 3
        WS = WStream()
        R1 = RY + 65536
        RG = RY + 65536 + 24576
        RM = RX + 40960
        RO_ = RY
        RU = RY + 49152
        RD = RY + 49152 + 16384
        r1n = [0]

        def decl_r1(key, kk, cols, src):
            WS.declare(key, "a", r1n[0] % 3, R1 + (r1n[0] % 3) * 8192, kk, cols, src)
            r1n[0] += 1
        decl_r1("B0", 8, 512, wB_d[0])
        decl_r1("B1", 8, 512, wB_d[1])
        WS.declare("G0", "g", 0, RG, 2, 512, wG_d[0])
        decl_r1("B2", 8, 512, wB_d[2])
        WS.declare("G1", "g", 1, RG + 2048, 2, 512, wG_d[1])
        decl_r1("B3", 8, 512, wB_d[3])
        WS.declare("G2", "g", 0, RG, 2, 512, wG_d[2])
        WS.declare("G3", "g", 1, RG + 2048, 2, 512, wG_d[3])
        for j in range(8):
            decl_r1("A%d" % j, 8, 384, wA_d[j])
        for j in range(8):
            WS.declare("M%d" % j, "m", j % 3, RM + (j % 3) * 8192, 8, 512, wM_d[j])
        WS.declare("O", "o", 0, RO_, 8, 1024, wO_d[:, :], gate="ya_dead")
        un = [0]
        dn = [0]
        for hf in range(2):
            for j in range(NFF):
                WS.declare("U%d_%d" % (hf, j), "u", un[0] % 4, RU + (un[0] % 4) * 4096, 8, 256, wU_d[j], gate="merge_done")
                un[0] += 1
            if hf == 0:
                for d in range(8):
                    WS.declare("D0_%d" % d, "d", d % 2, RD + (d % 2) * 6144, 24, 128, wD_d[d], gate="phaseO_done")
            else:
                d1slots = [("d", 0, RD), ("d", 1, RD + 6144), ("e", 0, RH), ("e", 1, RH + 6144),
                           ("f", 0, RU), ("f", 1, RU + 6144), ("d", 0, RD), ("d", 1, RD + 6144)]
                for d in range(8):
                    ring, slot, off = d1slots[d]
                    WS.declare("D1_%d" % d, ring, slot, off, 24, 128, wD_d[d], gate=(None if d < 2 else "up1_done"))

        XL = RX
        XLB0 = RX + 32768
        XLB1 = RY + 24576
        SET0 = RX + 40960
        SET1 = RY
        HALO3 = E_HALO
        bslot = [0]

        def next_bslot():
            s_ = bslot[0] % 4
            bslot[0] += 1
            return s_ * 1024

        def half_tile(col0, wget, wc0, rhs, K, hf, name=""):
            for n in range(2):
                mms = []
                for k in range(K):
                    mms.append((B.ps(col0 + n * 512, 512), wget(k, wc0, wc0 + 128),
                                rhs(k, hf * HW_ + n * 512, hf * HW_ + (n + 1) * 512), k == 0, k == K - 1))
                mm_group(mms, name=name)

        def xlv(j):
            return B.sb(XL + (j % 4) * 8192, F32, S)

        def xlbv(j):
            return B.sb((XLB0 if (j // 2) % 2 == 0 else XLB1) + (j % 2) * 4096, BF16, S)

        def bset(j):
            base = SET0 if j % 2 == 0 else SET1
            return dict(A=[B.sb(base + hf * 12288, F32, HW_) for hf in range(2)],
                        Bm=[B.sb(base + hf * 12288 + 4096, F32, HW_) for hf in range(2)],
                        C=[B.sb(base + hf * 12288 + 8192, F32, HW_) for hf in range(2)])
        lxP = {}

        def lx_pe(j):
            wget = WS.use("B%d" % (j // 2))
            c = j % 2
            assert bslot[0] % 2 == 0
            for hf in range(2):
                col = next_bslot()
                half_tile(col, wget, c * 128, hv, 8, hf, name="lx%d" % j)
                if hf == 0:
                    lxP[j] = col

        def lx_act(j, hf):
            xl = xlv(j)
            a, b = hf * HW_, (hf + 1) * HW_
            act(xl.sl(a, b), B.ps(lxP[j] + a, HW_), AF.Identity, bias=cv(C_LCB + j), scale=cv(C_LCW + 3 * 8 + j), name="lconv_t3")

        def lx_dve(j):
            xl = xlv(j)
            xb = xlbv(j)
            P = B.ps(lxP[j], S)
            for hf in range(2):
                a, b = hf * HW_, (hf + 1) * HW_
                for tap, sh in ((2, 1), (1, 2), (0, 3)):
                    lo = max(a, sh)
                    stt(xl.sl(lo, b), P.sl(lo - sh, b - sh), cv(C_LCW + tap * 8 + j), xl.sl(lo, b), ALU.mult, ALU.add,
                        name="lconv_t%d" % tap)
                if hf == 0:
                    act(xb.sl(a, b), xl.sl(a, b), AF.Copy, name="xl_bf")
                else:
                    B.add("dve", lambda e, o=xb.sl(a, b), i=xl.sl(a, b): e.tensor_copy(out=o.ap, in_=i.ap),
                          reads=[xl.sl(a, b)], writes=[xb.sl(a, b)], name="xl_bf")

        def gate_pe_tanh(j):
            hd, c = j // 2, j % 2
            gget = WS.use("G%d" % hd)

            def xlb_rhs(k, a, b):
                return xlbv(2 * hd + k).sl(a, b)
            st = bset(j)
            for hf in range(2):
                ca = next_bslot()
                half_tile(ca, gget, c * 128, xlb_rhs, 2, hf, name="za%d" % j)
                act(st["A"][hf], B.ps(ca, HW_), AF.Tanh, bias=cv(C_HBA + j), scale=0.5, name="r'")
                cx_ = next_bslot()
                half_tile(cx_, gget, 256 + c * 128, xlb_rhs, 2, hf, name="zx%d" % j)
                act(st["C"][hf], B.ps(cx_, HW_), AF.Tanh, bias=cv(C_HBX + j), scale=0.5, name="i'")
            if c == 1:
                WS.done("G%d" % hd)

        def gate_exp_a2(j):
            st = bset(j)
            for hf in range(2):
                act(st["Bm"][hf], st["A"][hf], AF.Exp, bias=cv(C_LS8 + j), scale=cv(C_LS8 + j), name="a2")

        def gate_exp_a(j):
            st = bset(j)
            for hf in range(2):
                act(st["A"][hf], st["A"][hf], AF.Exp, bias=cv(C_LS4 + j), scale=cv(C_LS4 + j), name="a")

        def gate_sqrt(j):
            st = bset(j)
            for hf in range(2):
                act(st["Bm"][hf], st["Bm"][hf], AF.Sqrt, bias=0.25, scale=-0.25, name="mult/2")

        def gate_dve(j):
            st = bset(j)
            xl = xlv(j)
            memset(st["Bm"][0].sl(0, 1), 0.5, eng="dve", name="mult0")
            for hf in range(2):
                a, b = hf * HW_, (hf + 1) * HW_
                stt(st["C"][hf], st["C"][hf], 1.0, xl.sl(a, b), ALU.add, ALU.mult, name="(i'+1)*x")
            for hf in range(2):
                tt(st["C"][hf], st["C"][hf], st["Bm"][hf], ALU.mult, name="u")
            scan(st["Bm"][0], st["A"][0], st["C"][0], 0.0, name="scan0")
            scan(st["Bm"][1], st["A"][1], st["C"][1], st["Bm"][0].sl(HW_ - 1, HW_), name="scan1")

        def ly_pe_act(j):
            hd, c = j // 2, j % 2
            wget = WS.use("B%d" % hd)
            st = bset(j)
            for hf in range(2):
                cy = next_bslot()
                half_tile(cy, wget, 256 + c * 128, hv, 8, hf, name="ly%d" % j)
                act(st["C"][hf], B.ps(cy, HW_), AF.Gelu_apprx_tanh, name="gelu_ly")
            if c == 1:
                WS.done("B%d" % hd)

        def yb_dve(j):
            st = bset(j)
            for hf in range(2):
                a, b = hf * HW_, (hf + 1) * HW_
                tt(ybv(j, a, b), st["Bm"][hf], st["C"][hf], ALU.mult, name="yb", eng=YB_ENG)

        for j in range(-2, 9):
            cur = 0 <= j < 8
            nxt = 0 <= j + 2 < 8
            prv = 0 <= j - 1 < 8
            if cur:
                gate_pe_tanh(j)
            if nxt:
                lx_pe(j + 2)
            if prv:
                gate_dve(j - 1)
            if cur:
                gate_exp_a2(j)
            if nxt:
                lx_act(j + 2, 0)
            if cur:
                gate_exp_a(j)
            if nxt:
                lx_act(j + 2, 1)
            if cur:
                gate_sqrt(j)
            if prv:
                ly_pe_act(j - 1)
            if nxt:
                lx_dve(j + 2)
            if prv:
                yb_dve(j - 1)

        SC_A = RX
        for j in range(8):
            wget = WS.use("A%d" % j)
            sb_ = SC_A + (j % 2) * 16384
            cxq = B.sb(sb_, F32, S)
            pb = B.sb(sb_ + 8192, F32, S)
            c0 = next_slot()
            full_tile(c0, wget, 0, hv, 8, name="cx%d" % j)
            for hf in range(2):
                a, b = hf * HW_, (hf + 1) * HW_
                act(cxq.sl(a, b), B.ps(c0 + a, HW_), AF.Copy, name="cxcopy")
            c1 = next_slot()
            full_tile(c1, wget, 128, hv, 8, name="cc%d" % j)
            for hf in range(2):
                a, b = hf * HW_, (hf + 1) * HW_
                tt(pb.sl(a, b), B.ps(c1 + a, HW_), cxq.sl(a, b), ALU.mult, name="p")
            act(cxq, pb, AF.Identity, scale=cv(C_CSW + 2 * 8 + j), name="conv_t2")
            stt(cxq.sl(1, S), pb.sl(0, S - 1), cv(C_CSW + 1 * 8 + j), cxq.sl(1, S), ALU.mult, ALU.add, name="conv_t1")
            stt(cxq.sl(2, S), pb.sl(0, S - 2), cv(C_CSW + 0 * 8 + j), cxq.sl(2, S), ALU.mult, ALU.add, name="conv_t0")
            c2 = next_slot()
            full_tile(c2, wget, 256, hv, 8, name="cb%d" % j)
            for hf in range(2):
                a, b = hf * HW_, (hf + 1) * HW_
                tt(yav(j, a, b), B.ps(c2 + a, HW_), cxq.sl(a, b), ALU.mult, name="ya")
            WS.done("A%d" % j)

        SC_M = RX
        for j in range(8):
            wget = WS.use("M%d" % j)
            cgc = next_slot()
            full_tile(cgc, wget, 0, hv, 8, name="gc%d" % j)
            S1 = [B.sb(SC_M + (j % 2) * 16384 + hf * 4096, F32, HW_) for hf in range(2)]
            S2 = [B.sb(SC_M + (j % 2) * 16384 + 8192 + hf * 4096, F32, HW_) for hf in range(2)]
            for hf in range(2):
                act(S1[hf], B.ps(cgc + hf * HW_, HW_), AF.Sigmoid, name="sg_conv")
            cba = next_slot()
            full_tile(cba, wget, 128, yav, 8, name="brA%d" % j)
            if j == 7:
                WS.open_gate("ya_dead")
            for hf in range(2):
                tt(S1[hf], B.ps(cba + hf * HW_, HW_), S1[hf], ALU.mult, name="mA")
            cgl = next_slot()
            full_tile(cgl, wget, 256, hv, 8, name="gl%d" % j)
            for hf in range(2):
                act(S2[hf], B.ps(cgl + hf * HW_, HW_), AF.Sigmoid, name="sg_lru")
            cbb = next_slot()
            full_tile(cbb, wget, 384, ybv, 8, name="brB%d" % j)
            for hf in range(2):
                a, b = hf * HW_, (hf + 1) * HW_
                tt(S2[hf], B.ps(cbb + a, HW_), S2[hf], ALU.mult, name="mB")
                tt(mgv(j, a, b), S1[hf], S2[hf], ALU.add, name="merged")
            WS.done("M%d" % j)

        WS.open_gate("merge_done")
        woget = WS.use("O")
        MIXB = RY + 16384
        _, _, sdO, rsO = norm_scratch(RH + 16384)
        for q in range(4):
            dma("sp", B.sb3(RX, F32, 8, S, q * QW, (q + 1) * QW), xT3[:, :, q * QW:(q + 1) * QW], "x%d" % q, name="xreload%d" % q)
        bank_i = [0]
        EARLY_H2 = [None]

        def sq_act(src):
            i = nsq[0] % 4
            nsq[0] += 1
            act(sq_r[i], src, AF.Square, name="sq")
            return sq_r[i]

        def mixv(q, j):
            return B.sb(MIXB + (q % 2) * 16384 + j * 2048, F32, QW)

        ss2_pend = [None]

        def ss2_flush():
            if ss2_pend[0] is not None:
                q_, j_, sqv = ss2_pend[0]
                ss2 = B.ps(7 * 512, QW)
                mm_group([(ss2, ones_r, sqv, j_ == 0, j_ == 7)], name="ss2mm")
                if j_ == 7:
                    a_, b_ = q_ * QW, (q_ + 1) * QW
                    rstd_act(rstd2.sl(a_, b_), sdO[1], ss2, name="rstd2")
                ss2_pend[0] = None

        def ss2_piece(q, j):
            ss2_flush()
            a, b = q * QW, (q + 1) * QW
            ss2_pend[0] = (q, j, sq_act(xv(j, a, b)))

        def mix_stage(q):
            a, b = q * QW, (q + 1) * QW
            ss1 = B.ps(6 * 512, QW)
            pend = None
            for j in range(8):
                pcol = (bank_i[0] % 6) * 512
                bank_i[0] += 1
                P = B.ps(pcol, QW)
                mm_group([(P, woget(k, j * 128, (j + 1) * 128), mgv(k, a, b), k == 0, k == 7) for k in range(8)], name="mix%d" % j)
                if pend is not None:
                    mm_group([(ss1, ones_r, pend[0], pend[1] == 0, False)], name="ss1mm")
                ss2_flush()
                act(mixv(q, j), P, AF.Copy, name="mixcopy")
                pend = (sq_act(P), j)
                if q >= 1:
                    ss2_piece(q - 1, j)
            mm_group([(ss1, ones_r, pend[0], False, True)], name="ss1mm")
            ss2_flush()

        def norm_stage(q):
            a, b = q * QW, (q + 1) * QW
            ss1 = B.ps(6 * 512, QW)
            rstd_act(rsO[0], sdO[0], ss1, name="rstdO")
            for j in range(8):
                stt(mixv(q, j), mixv(q, j), cv(C_G2 + j), rsO[0], ALU.mult, ALU.mult, name="mixn")
                tt(xv(j, a, b), mixv(q, j), xv(j, a, b), ALU.add, name="x1")

        def _early_h2():
            for k in range(8):
                stt(B.sb(RH + k * 2048, BF16, HW_), xv(k, 0, HW_), cv(C_G3 + k), rstd2.sl(0, HW_), ALU.mult, ALU.mult, name="h2")
        EARLY_H2[0] = _early_h2
        for q in range(3):
            mix_stage(q)
            norm_stage(q)
        EARLY_H2[0]()
        mix_stage(3)
        WS.done("O")
        norm_stage(3)

        H2 = RH
        OSB = RH + 16384
        FB = RY
        GV = RY + 49152 + 16384 + 12288
        FSC = GV + 16384
        _, _, sdF, rsF = norm_scratch(FSC, 2, 1)
        assert FSC + 4096 <= RE
        HALO = E_HALO
        bank_u = [0]
        bank_d = [0]
        uidx = [0]
        didx = [0]
        yT3 = yT.rearrange("(k p) t -> p k t", p=128)

        def h2_stage(hf):
            a, b = hf * HW_, (hf + 1) * HW_
            for k in range(8):
                stt(B.sb(H2 + k * 2048, BF16, HW_), xv(k, a, b), cv(C_G3 + k), rstd2.sl(a, b), ALU.mult, ALU.mult, name="h2")

        def h2v(k, c0, c1):
            return B.sb(H2 + k * 2048, BF16, HW_).sl(c0, c1)

        def gvbuf(j, gi):
            return B.sb(GV + ((j % 2) * 2 + gi) * 4096, F32, HW_)

        def up_stage(hf, j):
            wget = WS.use("U%d_%d" % (hf, j))
            Ps = []
            for gi in range(2):
                pcol = ((bank_u[0] % 3) if bank_u[0] < 24 else (bank_u[0] % 4)) * 1024
                bank_u[0] += 1
                mms = []
                for n in range(2):
                    for k in range(8):
                        mms.append((B.ps(pcol + n * 512, 512), wget(k, gi * 128, gi * 128 + 128), h2v(k, n * 512, (n + 1) * 512),
                                    k == 0, k == 7))
                mm_group(mms, name="up%d_%d" % (j, gi))
                Ps.append(B.ps(pcol, HW_))
            WS.done("U%d_%d" % (hf, j))
            for gi in range(2):
                ch = j if gi == 0 else NFF + j
                P, t = Ps[gi], gvbuf(j, gi)
                halo = B.sb(HALO + ch * 8, F32, 2)
                w1, w0 = cv(C_FCW + 1 * 48 + ch), cv(C_FCW + 0 * 48 + ch)
                act(t, P, AF.Identity, bias=cv(C_FCB + ch), scale=cv(C_FCW + 2 * 48 + ch), name="fconv_t2")
                if hf == 0:
                    act(halo, P.sl(HW_ - 2, HW_), AF.Copy, name="halo")
                else:
                    act(t.sl(0, 1), halo.sl(1, 2), AF.Identity, bias=t.sl(0, 1), scale=w1, name="fconv_b1")
                    act(t.sl(0, 1), halo.sl(0, 1), AF.Identity, bias=t.sl(0, 1), scale=w0, name="fconv_b0a")
                    act(t.sl(1, 2), halo.sl(1, 2), AF.Identity, bias=t.sl(1, 2), scale=w0, name="fconv_b0b")
            for gi in range(2):
                ch = j if gi == 0 else NFF + j
                P, t = Ps[gi], gvbuf(j, gi)
                stt(t.sl(1, HW_), P.sl(0, HW_ - 1), cv(C_FCW + 1 * 48 + ch), t.sl(1, HW_), ALU.mult, ALU.add, name="fconv_t1")
                stt(t.sl(2, HW_), P.sl(0, HW_ - 2), cv(C_FCW + 0 * 48 + ch), t.sl(2, HW_), ALU.mult, ALU.add, name="fconv_t0")
            g, v = gvbuf(j, 0), gvbuf(j, 1)
            act(g, g, AF.Gelu_apprx_tanh, name="gelu_f")
            tt(B.sb(FB + j * 2048, BF16, HW_), g, v, ALU.mult, name="f")

        def fv(k, c0, c1):
            return B.sb(FB + k * 2048, BF16, HW_).sl(c0, c1)

        def odv(qq, d):
            return B.sb((OSB if qq == 0 else GV) + d * 2048, F32, QW)

        def down_stage(hf, order, mid_hook=None):
            ss3 = [B.ps(6 * 512, QW), B.ps(7 * 512, QW)]
            pend = None
            KS = 20
            seen_d = {}
            cnt = [0, 0]
            for (d, qq) in order:
                wget = WS.use("D%d_%d" % (hf, d))
                pcol = (bank_d[0] % 6) * 512
                bank_d[0] += 1
                P = B.ps(pcol, QW)
                mm_group([(P, wget(k, 0, 128), fv(k, qq * QW, (qq + 1) * QW), k == 0, False) for k in range(KS)],
                         name="down%d_%d a" % (d, qq))
                mm_group([(P, wget(k, 0, 128), fv(k, qq * QW, (qq + 1) * QW), False, k == NFF - 1) for k in range(KS, NFF)],
                         name="down%d_%d b" % (d, qq))
                if pend is not None:
                    mm_group([(ss3[pend[2]], ones_r, pend[0], pend[3] == 0, pend[3] == 7)], name="ss3mm")
                    if pend[3] == 7 and mid_hook is not None and pend[2] == 0:
                        mid_hook()
                act(odv(qq, d), P, AF.Copy, name="ocopy")
                pend = (sq_act(P), d, qq, cnt[qq])
                cnt[qq] += 1
                seen_d[d] = seen_d.get(d, 0) + 1
                if seen_d[d] == 2:
                    WS.done("D%d_%d" % (hf, d))
            mm_group([(ss3[pend[2]], ones_r, pend[0], pend[3] == 0, pend[3] == 7)], name="ss3mm")

        def post_rstd(qq, rs, tmp):
            ss3 = B.ps((6 + qq) * 512, QW)
            rstd_act(rs, tmp, ss3, name="rstdF")

        def post_apply(hf, qq, rs):
            q = 2 * hf + qq
            ga, gb = q * QW, (q + 1) * QW
            nsplit = 4 if q == 3 else 1
            per = 8 // nsplit
            for d in range(8):
                stt(odv(qq, d), odv(qq, d), cv(C_G4 + d), rs, ALU.mult, ALU.mult, name="on")
                tt(xv(d, ga, gb), odv(qq, d), xv(d, ga, gb), ALU.add, name="out")
                if (d + 1) % per == 0:
                    k0 = d + 1 - per
                    dma("sp", yT3[:, k0:d + 1, ga:gb], B.sb3(RX + k0 * 8192, F32, per, S, ga, gb), "out%d" % q,
                        name="store%d_%d" % (q, k0), out_is_dram=True)
            B.out_chans.append(("out%d" % q, B.chan_val["out%d" % q]))

        ORDER0 = [(d, qq) for d in range(8) for qq in range(2)]
        ORDER1 = [(0, 0), (1, 0), (2, 0), (3, 0), (4, 0), (0, 1), (5, 0), (1, 1), (6, 0), (2, 1), (7, 0), (3, 1),
                  (4, 1), (5, 1), (6, 1), (7, 1)]
        rs_def = B.sb(E_RSTD2, F32, QW)
        rs_tmp = B.sb(E_RSTD2 + 2048, F32, QW)
        for hf in range(2):
            for j in range(NFF):
                up_stage(hf, j)
                if hf == 0 and 2 <= j < 10:
                    ss2_piece(3, j - 2)
                if hf == 0 and j == 10:
                    ss2_flush()
                    WS.open_gate("phaseO_done")
            if hf == 0:
                h2_stage(1)
                down_stage(0, ORDER0)
                post_rstd(1, rsF[0], sdF[0])
                post_rstd(0, rs_def, rs_tmp)
                post_apply(0, 1, rsF[0])
            else:
                WS.open_gate("up1_done")
                post_apply(0, 0, rs_def)

                def _mid():
                    post_rstd(0, rsF[0], sdF[0])
                    post_apply(1, 0, rsF[0])
                down_stage(1, ORDER1, mid_hook=_mid)
                post_rstd(1, rsF[0], sdF[0])
                post_apply(1, 1, rsF[0])

        assert HALO + 48 * 8 <= ARENA_BYTES

        def sem_of(key):
            kind, nm = key
            return sem_eng[nm] if kind == "eng" else chan_sems[nm]

        def emit_engine(e, name):
            for op in B.ops[name]:
                for key, val in op.waits:
                    e.wait_ge(sem_of(key), val)
                inst = op.emit(e)
                if op.is_dma:
                    inst.then_inc(chan_sems[op.chan], 16)
                else:
                    inst.then_inc(sem_eng[name], 1)

        if debug == "deps":
            return B
        block = es.enter_context(nc.Block())

        @block.sync
        def _(e):
            emit_engine(e, "sp")
            for c, v in B.out_chans:
                e.wait_ge(chan_sems[c], v)

        @block.gpsimd
        def _(e):
            emit_engine(e, "pool")

        @block.tensor
        def _(e):
            emit_engine(e, "pe")

        @block.scalar
        def _(e):
            emit_engine(e, "act")

        @block.vector
        def _(e):
            emit_engine(e, "dve")
    return nc


def _tile_w(w):
    K, C = w.shape
    return np.ascontiguousarray(w.reshape(K // 128, 128, C).transpose(1, 0, 2))


def _chunkcols(v, n):
    return np.ascontiguousarray(np.asarray(v, np.float32).reshape(n, 128).T)


def prepare_inputs(x, norm_mix_pre, norm_mix_post, norm_ffn_pre, norm_ffn_post, w_in, conv_short_w,
                   w_conv_branch, lru_conv_w, lru_conv_b, lru_wa, lru_ba, lru_wx, lru_bx, lru_lambda,
                   w_lru_branch, w_out, ffn_w_up, ffn_conv_w, ffn_conv_b, ffn_w_down):
    f = np.float32
    w_in = np.asarray(w_in[0], f)
    cb, cc, cx = w_in[:, 0:1024], w_in[:, 1024:2048], w_in[:, 2048:3072]
    lx, ly = w_in[:, 3072:4096], w_in[:, 4096:5120]
    gconv, glru = w_in[:, 5120:6144], w_in[:, 6144:7168]
    wcb = np.asarray(w_conv_branch[0], f)
    wlb = np.asarray(w_lru_branch[0], f)

    def ch(m, j):
        return m[:, j * 128:(j + 1) * 128]
    wA = np.stack([_tile_w(np.concatenate([ch(cx, j), ch(cc, j), ch(cb, j)], axis=1)).reshape(128, -1) for j in range(8)])
    wB = np.stack([_tile_w(np.concatenate([ch(lx, 2 * h), ch(lx, 2 * h + 1), ch(ly, 2 * h), ch(ly, 2 * h + 1)], axis=1)).reshape(128, -1)
                   for h in range(4)])
    wa = np.asarray(lru_wa[0], f)
    wx = np.asarray(lru_wx[0], f)
    wG = np.stack([_tile_w(np.concatenate([wa[h], wx[h]], axis=1)).reshape(128, -1) for h in range(4)])
    wM = np.stack([_tile_w(np.concatenate([ch(gconv, j), ch(wcb, j), ch(glru, j), ch(wlb, j)], axis=1)).reshape(128, -1)
                   for j in range(8)])
    wO = _tile_w(np.asarray(w_out[0], f)).reshape(128, -1)
    wup = np.asarray(ffn_w_up[0], f)
    gate, val = wup[:, 0:3072], wup[:, 3072:6144]
    wU = np.stack([_tile_w(np.concatenate([ch(gate, j), ch(val, j)], axis=1)).reshape(128, -1) for j in range(24)])
    wdn = np.asarray(ffn_w_down[0], f)
    wD = np.stack([_tile_w(ch(wdn, d)).reshape(128, -1) for d in range(8)])

    cvec = np.zeros((128, NV), f)
    cvec[:, C_G1:C_G1 + 8] = _chunkcols(norm_mix_pre[0], 8)
    cvec[:, C_G2:C_G2 + 8] = _chunkcols(norm_mix_post[0], 8)
    cvec[:, C_G3:C_G3 + 8] = _chunkcols(norm_ffn_pre[0], 8)
    cvec[:, C_G4:C_G4 + 8] = _chunkcols(norm_ffn_post[0], 8)
    for t in range(3):
        cvec[:, C_CSW + t * 8:C_CSW + t * 8 + 8] = _chunkcols(conv_short_w[0][t], 8)
    for t in range(4):
        cvec[:, C_LCW + t * 8:C_LCW + t * 8 + 8] = _chunkcols(lru_conv_w[0][t], 8)
    cvec[:, C_LCB:C_LCB + 8] = _chunkcols(lru_conv_b[0], 8)
    cvec[:, C_LBA:C_LBA + 8] = _chunkcols(np.asarray(lru_ba[0]).reshape(-1), 8)
    cvec[:, C_LBX:C_LBX + 8] = _chunkcols(np.asarray(lru_bx[0]).reshape(-1), 8)
    cvec[:, C_LAM:C_LAM + 8] = _chunkcols(lru_lambda[0], 8)
    for t in range(3):
        cvec[:, C_FCW + t * 48:C_FCW + t * 48 + 48] = _chunkcols(ffn_conv_w[0][t], 48)
    cvec[:, C_FCB:C_FCB + 48] = _chunkcols(ffn_conv_b[0], 48)
    shared = dict(cvec=cvec, wA=wA, wB=wB, wG=wG, wM=wM, wO=wO, wU=wU, wD=wD)
    x = np.asarray(x, f)
    in_maps = []
    for b in range(NCORES):
        m = dict(shared)
        m["xT"] = np.ascontiguousarray(x[b].T)
        in_maps.append(m)
    return in_maps


def kernel(**inputs):
    in_maps = prepare_inputs(**inputs)
    nc = build_program()
    res = run_bass_kernel_spmd(nc, in_maps, core_ids=list(range(NCORES)))
    out = np.stack([np.ascontiguousarray(res.results[b]["yT"].T) for b in range(NCORES)], axis=0)
    return out.astype(np.float32)
```

```python
import numpy as np
import concourse.bass as bass
import concourse.mybir as mybir
from concourse.bass_utils import run_bass_kernel_spmd

F32 = mybir.dt.float32
F32R = mybir.dt.float32r
BF16 = mybir.dt.bfloat16
AF = mybir.ActivationFunctionType
ALU = mybir.AluOpType

D = 1024
S = 2048
NCH = 8
NFF = 24
NCORES = 8
EPS = 1e-6
QW = 512
HW_ = 1024

C_G1, C_G2, C_G3, C_G4 = 0, 8, 16, 24
C_CSW = 32
C_LCW = 56
C_LCB = 88
C_LBA = 96
C_LBX = 104
C_LAM = 112
C_FCW = 120
C_FCB = 264
C_LS8 = 312
C_LS4 = 320
C_HBA = 328
C_HBX = 336
C_TMP = 344
NV = 360
assert NV * 4 <= 1536
YB_ENG = "dve"

RX = 0
RH = 65536
RY = 98304
RE = 196608
E_CV = RE
E_ONES = E_CV + 1536
E_RSTD2 = E_ONES + 256
E_HALO = E_RSTD2 + 8192
E_SQ = E_HALO + 512
E_END = E_SQ + 4096
ARENA_BYTES = E_END
BLK = 256


class View:
    __slots__ = ("space", "ap", "ranges", "esz", "lo")

    def __init__(self, space, ap, ranges, esz, lo):
        self.space, self.ap, self.ranges, self.esz, self.lo = space, ap, ranges, esz, lo

    def sl(self, a, b):
        lo = self.lo + a * self.esz
        return View(self.space, self.ap[:, a:b], [(lo, self.lo + b * self.esz)], self.esz, lo)


class Op:
    __slots__ = ("eng", "emit", "reads", "writes", "count", "waits", "chan", "dma_val", "is_dma", "name", "dbg")


class Builder:
    def __init__(self, nc, arena, psum, rarena):
        self.nc = nc
        self.arena = arena
        self.psum = psum
        self.rarena = rarena
        self.ops = {e: [] for e in ("pe", "act", "dve", "pool", "sp")}
        self.ncompute = {e: 0 for e in ("pe", "act", "dve", "pool", "sp")}
        self.lastw = {"sb": {}, "ps": {}, "sr": {}, "tk": {}}
        self.readers = {"sb": {}, "ps": {}, "sr": {}, "tk": {}}
        self.seen = {e: {} for e in ("pe", "act", "dve", "pool", "sp")}
        self.chan_val = {}
        self.out_chans = []

    def sb(self, off, dtype, n):
        esz = 2 if dtype == BF16 else 4
        assert off % 4 == 0 and (n * esz) % 4 == 0
        a = self.arena[:, off // 4:(off + n * esz) // 4]
        if dtype != F32:
            a = a.bitcast(dtype)
        return View("sb", a, [(off, off + n * esz)], esz, off)

    def sb3(self, off, dtype, k, n, c0, c1, kstride=None):
        esz = 2 if dtype == BF16 else 4
        a = self.arena[:, off // 4:(off + k * n * esz) // 4]
        if dtype != F32:
            a = a.bitcast(dtype)
        a = a.rearrange("p (k n) -> p k n", k=k)[:, :, c0:c1]
        rg = [(off + (i * n + c0) * esz, off + (i * n + c1) * esz) for i in range(k)]
        return View("sb", a, rg, esz, off)

    def sr(self, e0, n):
        return View("sr", self.rarena[:, e0:e0 + n], [(e0 * 4, (e0 + n) * 4)], 4, e0 * 4)

    def tok(self, i):
        return View("tk", None, [(i * BLK, (i + 1) * BLK)], 4, i * BLK)

    def ps(self, col0, n):
        return View("ps", self.psum[:, col0:col0 + n], [(col0 * 4, (col0 + n) * 4)], 4, col0 * 4)

    @staticmethod
    def _blocks(v):
        for lo, hi in v.ranges:
            for b in range(lo // BLK, (hi - 1) // BLK + 1):
                yield b

    def add(self, eng, emit, reads=(), writes=(), chan=None, name=""):
        op = Op()
        op.eng, op.emit, op.name = eng, emit, name
        op.is_dma = chan is not None
        op.chan = chan
        deps = {}
        for v in reads:
            lw = self.lastw[v.space]
            for b in self._blocks(v):
                w = lw.get(b)
                if w is not None:
                    deps[id(w)] = w
        for v in writes:
            lw = self.lastw[v.space]
            rd = self.readers[v.space]
            for b in self._blocks(v):
                w = lw.get(b)
                if w is not None:
                    deps[id(w)] = w
                r = rd.get(b)
                if r:
                    for o in r.values():
                        deps[id(o)] = o
        for v in reads:
            rd = self.readers[v.space]
            for b in self._blocks(v):
                r = rd.get(b)
                if r is None:
                    r = rd[b] = {}
                key = eng if not op.is_dma else ("dma", id(op))
                r[key] = op
        for v in writes:
            lw = self.lastw[v.space]
            rd = self.readers[v.space]
            for b in self._blocks(v):
                lw[b] = op
                rd[b] = {}
        if op.is_dma:
            self.chan_val[chan] = self.chan_val.get(chan, 0) + 16
            op.dma_val = self.chan_val[chan]
            op.count = None
        else:
            self.ncompute[eng] += 1
            op.count = self.ncompute[eng]
        waits = []
        seen = self.seen[eng]
        for d in deps.values():
            if d is op:
                continue
            if d.is_dma:
                key, val = ("chan", d.chan), d.dma_val
            else:
                if d.eng == "pe" and eng == "pe" and not op.is_dma:
                    continue
                key, val = ("eng", d.eng), d.count
            if seen.get(key, 0) < val:
                seen[key] = val
                waits.append((key, val))
        best = {}
        for key, val in waits:
            if best.get(key, 0) < val:
                best[key] = val
        op.waits = list(best.items())
        op.dbg = [(d.name, d.eng, d.count if not d.is_dma else d.dma_val) for d in deps.values()]
        self.ops[eng].append(op)
        return op


def build_program(debug=None):
    nc = bass.Bass("TRN2", target_bir_lowering=False)
    xT = nc.dram_tensor("xT", [D, S], F32, kind="ExternalInput").ap()
    cvec_d = nc.dram_tensor("cvec", [128, NV], F32, kind="ExternalInput").ap()
    wA_d = nc.dram_tensor("wA", [8, 128, 8 * 384], F32, kind="ExternalInput").ap()
    wB_d = nc.dram_tensor("wB", [4, 128, 8 * 512], F32, kind="ExternalInput").ap()
    wG_d = nc.dram_tensor("wG", [4, 128, 2 * 512], F32, kind="ExternalInput").ap()
    wM_d = nc.dram_tensor("wM", [8, 128, 8 * 512], F32, kind="ExternalInput").ap()
    wO_d = nc.dram_tensor("wO", [128, 8 * 1024], F32, kind="ExternalInput").ap()
    wU_d = nc.dram_tensor("wU", [24, 128, 8 * 256], F32, kind="ExternalInput").ap()
    wD_d = nc.dram_tensor("wD", [8, 128, 24 * 128], F32, kind="ExternalInput").ap()
    yT = nc.dram_tensor("yT", [D, S], F32, kind="ExternalOutput").ap()
    dbg_d = None
    if debug:
        dbg_d = nc.dram_tensor("dbg", [128, 8 * 2048], F32, kind="ExternalOutput").ap()

    from contextlib import ExitStack
    with ExitStack() as es:
        arena = es.enter_context(nc.sbuf_tensor("arena", [128, ARENA_BYTES // 4], F32))
        psum = es.enter_context(nc.psum_tensor("psum", [128, 4096], F32))
        B = Builder(nc, arena, psum, None)
        sem_eng = {e: es.enter_context(nc.semaphore("sem_" + e)) for e in ("pe", "act", "dve", "pool")}
        chan_sems = {}

        def chan_sem(c):
            if c not in chan_sems:
                chan_sems[c] = es.enter_context(nc.semaphore("ch_" + c))
            return chan_sems[c]

        def dma(queue, out_view, in_ap, chan, name="", out_is_dram=False, reads=(), writes=()):
            chan_sem(chan)
            if out_is_dram:
                return B.add(queue, lambda e, o=out_view, i=in_ap: e.dma_start(out=o, in_=i.ap),
                             reads=[in_ap], writes=[], chan=chan, name=name)
            return B.add(queue, lambda e, o=out_view, i=in_ap: e.dma_start(out=o.ap, in_=i),
                         reads=list(reads), writes=[out_view] + list(writes), chan=chan, name=name)

        def act(out, in_, func, bias=None, scale=1.0, name="", extra_reads=()):
            kw = {}
            rd = [in_] + list(extra_reads)
            if bias is not None:
                if isinstance(bias, View):
                    kw["bias"] = bias.ap
                    rd.append(bias)
                else:
                    kw["bias"] = bias
            if isinstance(scale, View):
                rd.append(scale)
                sc = scale.ap
            else:
                sc = scale
            return B.add("act", lambda e, o=out, i=in_, f=func, sc=sc, kw=kw: e.activation(out=o.ap, in_=i.ap, func=f, scale=sc, **kw),
                         reads=rd, writes=[out], name=name)

        def tt(out, in0, in1, op, name="", eng="dve"):
            return B.add(eng, lambda e, o=out, a=in0, b=in1, op=op: e.tensor_tensor(out=o.ap, in0=a.ap, in1=b.ap, op=op),
                         reads=[in0, in1], writes=[out], name=name)

        def stt(out, in0, scalar, in1, op0, op1, name=""):
            rd = [in0, in1]
            if isinstance(scalar, View):
                rd.append(scalar)
                sc = scalar.ap
            else:
                sc = scalar
            return B.add("dve", lambda e, o=out, a=in0, sc=sc, b=in1, op0=op0, op1=op1: e.scalar_tensor_tensor(
                out=o.ap, in0=a.ap, scalar=sc, in1=b.ap, op0=op0, op1=op1), reads=rd, writes=[out], name=name)

        def ts(out, in0, s1, s2, op0, op1=None, name="", eng="dve"):
            rd = [in0]
            a1 = s1.ap if isinstance(s1, View) else s1
            a2 = s2.ap if isinstance(s2, View) else s2
            if isinstance(s1, View):
                rd.append(s1)
            if isinstance(s2, View):
                rd.append(s2)
            if op1 is None:
                return B.add(eng, lambda e, o=out, a=in0, a1=a1, op0=op0: e.tensor_scalar(
                    out=o.ap, in0=a.ap, scalar1=a1, scalar2=None, op0=op0), reads=rd, writes=[out], name=name)
            return B.add(eng, lambda e, o=out, a=in0, a1=a1, a2=a2, op0=op0, op1=op1: e.tensor_scalar(
                out=o.ap, in0=a.ap, scalar1=a1, scalar2=a2, op0=op0, op1=op1), reads=rd, writes=[out], name=name)

        def rstd_act(out, tmp, ss, name=""):
            act(tmp, ss, AF.Ln, bias=EPS, scale=1.0 / D, name=name + "_ln")
            act(out, tmp, AF.Exp, scale=-0.5, name=name)

        def recip(out, in_, name=""):
            return B.add("dve", lambda e, o=out, i=in_: e.reciprocal(out=o.ap, in_=i.ap), reads=[in_], writes=[out], name=name)

        def memset(out, val, eng="dve", name=""):
            return B.add(eng, lambda e, o=out, v=val: e.memset(o.ap, v), reads=[], writes=[out], name=name)

        def scan(out, d0, d1, init, name=""):
            rd = [d0, d1]
            if isinstance(init, View):
                rd.append(init)
                ini = init.ap
            else:
                ini = init
            return B.add("dve", lambda e, o=out, a=d0, b=d1, ini=ini: e.tensor_tensor_scan(
                out=o.ap, data0=a.ap, data1=b.ap, initial=ini, op0=ALU.mult, op1=ALU.add), reads=rd, writes=[out], name=name)

        def mm_group(mms, name=""):
            reads, writes = [], []
            for (o, l, r, st, sp) in mms:
                reads.append(l)
                reads.append(r)
                writes.append(o)

            def emit(e, mms=mms):
                inst = None
                for (o, l, r, st, sp) in mms:
                    inst = e.matmul(o.ap, l.ap, r.ap, start=st, stop=sp)
                return inst
            return B.add("pe", emit, reads=reads, writes=writes, name=name)

        def cv(col, n=1):
            return B.sb(E_CV + col * 4, F32, n)
        ones_r = B.sb(E_ONES, BF16, 128)
        sq_r = [B.sb(E_SQ + i * 1024, BF16, QW) for i in range(4)]
        rstd2 = B.sb(E_RSTD2, F32, S)

        def xv(k, c0=0, c1=S):
            return B.sb(RX + k * 8192, F32, S).sl(c0, c1)

        def hv(k, c0=0, c1=S):
            return B.sb(RH + k * 4096, BF16, S).sl(c0, c1)

        def yav(k, c0=0, c1=S):
            return B.sb(RY + k * 4096, BF16, S).sl(c0, c1)

        def ybv(k, c0=0, c1=S):
            return B.sb(RY + 32768 + k * 4096, BF16, S).sl(c0, c1)

        def mgv(k, c0=0, c1=S):
            return B.sb(RY + 65536 + k * 4096, BF16, S).sl(c0, c1)

        def wtile(off, kk, cols):
            full = B.sb(off, BF16, kk * cols)

            def get(k, c0, c1):
                return full.sl(k * cols + c0, k * cols + c1)
            return full, get

        dma("sp", cv(0, NV), cvec_d[:, :], "cv", name="cvec")
        memset(ones_r, 1.0, eng="dve", name="ones")
        act(cv(C_TMP, 8), cv(C_LAM, 8), AF.Exp, scale=-1.0, name="exp(-lam)")
        act(cv(C_TMP + 8, 8), cv(C_TMP, 8), AF.Ln, bias=1.0, name="ln1p")
        ts(cv(C_LS8, 8), cv(C_TMP + 8, 8), -8.0, None, ALU.mult, name="ls8")
        ts(cv(C_LS4, 8), cv(C_TMP + 8, 8), -4.0, None, ALU.mult, name="ls4")
        ts(cv(C_HBA, 16), cv(C_LBA, 16), 0.5, None, ALU.mult, name="half_biases")

        xT3 = xT.rearrange("(k p) t -> p k t", p=128)
        for q in range(4):
            dma("sp", B.sb3(RX, F32, 8, S, q * QW, (q + 1) * QW), xT3[:, :, q * QW:(q + 1) * QW], "x%d" % q, name="xload%d" % q,
                writes=[B.tok(q)])

        def norm_scratch(base, nsq=4, nsd=2):
            sq = sqr = sq_r
            o = base
            sd = [B.sb(o + i * 2048, F32, QW) for i in range(nsd)]
            o += nsd * 2048
            rs = [B.sb(o + i * 2048, F32, QW) for i in range(nsd)]
            return sq, sqr, sd, rs

        sq0, sq0r, sd0, rs0 = norm_scratch(RY)
        nsq = [0]

        def sumsq_chunk(src_view, ss_ps, first, last, sqs, sqrs):
            i = nsq[0] % len(sqs)
            nsq[0] += 1
            act(sqrs[i], src_view, AF.Square, name="sq")
            mm_group([(ss_ps, ones_r, sqrs[i], first, last)], name="ssmm")

        for q in range(4):
            ss = B.ps((4 + q) * 512, QW)
            for k in range(8):
                sumsq_chunk(xv(k, q * QW, (q + 1) * QW), ss, k == 0, k == 7, sq0, sq0r)
            rstd_act(rs0[q % 2], sd0[q % 2], ss, name="rstd1")
            for k in range(8):
                stt(hv(k, q * QW, (q + 1) * QW), xv(k, q * QW, (q + 1) * QW), cv(C_G1 + k), rs0[q % 2],
                    ALU.mult, ALU.mult, name="h")

        slot_i = [0]

        def next_slot():
            s = slot_i[0] % 2
            slot_i[0] += 1
            return s * 2048

        def full_tile(col0, wget, wc0, rhs, K, name=""):
            for hf in range(2):
                mms = []
                for n in (2 * hf, 2 * hf + 1):
                    for k in range(K):
                        mms.append((B.ps(col0 + n * 512, 512), wget(k, wc0, wc0 + 128), rhs(k, n * 512, (n + 1) * 512),
                                    k == 0, k == K - 1))
                mm_group(mms, name=name)

        class WStream:
            def __init__(self):
                self.tiles = []
                self.issued = 0
                self.slot_busy = {}
                self.index = {}
                self.gates = set()
                self.last_use = 0

            def declare(self, key, ring, slot, off, kk, cols, src, gate=None):
                self.index[key] = len(self.tiles)
                self.tiles.append(dict(key=key, off=off, kk=kk, cols=cols, src=src, chan="w%s%d" % (ring, slot),
                                       slotkey=(ring, slot), gate=gate))

            def open_gate(self, g):
                self.gates.add(g)
                self.pump(upto=self.last_use + LOOKAHEAD)

            def pump(self, upto=None):
                while self.issued < len(self.tiles):
                    t = self.tiles[self.issued]
                    if t["slotkey"] in self.slot_busy:
                        break
                    if t["gate"] is not None and t["gate"] not in self.gates:
                        break
                    if upto is not None and self.issued > upto:
                        break
                    full, get = wtile(t["off"], t["kk"], t["cols"])
                    dma("pool", full, t["src"], t["chan"], name="w_" + t["key"],
                        reads=([B.tok(0)] if t["key"] == "B0" else [B.tok(3)] if t["key"] == "B1" else []))
                    t["get"] = get
                    self.slot_busy[t["slotkey"]] = self.issued
                    self.issued += 1

            def use(self, key):
                n = self.index[key]
                self.last_use = max(self.last_use, n)
                if n >= self.issued:
                    self.pump(upto=n)
                assert n < self.issued, "weight tile %s cannot be issued (slot busy)" % key
                self.pump(upto=n + LOOKAHEAD)
                return self.tiles[n]["get"]

            def done(self, key):
                n = self.index[key]
                t = self.tiles[n]
                assert self.slot_busy.get(t["slotkey"]) == n
                del self.slot_busy[t["slotkey"]]
                self.pump(upto=n + LOOKAHEAD)

        LOOKAHEAD = 3
        WS = WStream()
        R1 = RY + 65536
        RG = RY + 65536 + 24576
        RM = RX + 40960
        RO_ = RY
        RU = RY + 49152
        RD = RY + 49152 + 16384
        r1n = [0]

        def decl_r1(key, kk, cols, src):
            WS.declare(key, "a", r1n[0] % 3, R1 + (r1n[0] % 3) * 8192, kk, cols, src)
            r1n[0] += 1
        decl_r1("B0", 8, 512, wB_d[0])
        decl_r1("B1", 8, 512, wB_d[1])
        WS.declare("G0", "g", 0, RG, 2, 512, wG_d[0])
        decl_r1("B2", 8, 512, wB_d[2])
        WS.declare("G1", "g", 1, RG + 2048, 2, 512, wG_d[1])
        decl_r1("B3", 8, 512, wB_d[3])
        WS.declare("G2", "g", 0, RG, 2, 512, wG_d[2])
        WS.declare("G3", "g", 1, RG + 2048, 2, 512, wG_d[3])
        for j in range(8):
            decl_r1("A%d" % j, 8, 384, wA_d[j])
        for j in range(8):
            WS.declare("M%d" % j, "m", j % 3, RM + (j % 3) * 8192, 8, 512, wM_d[j])
        WS.declare("O", "o", 0, RO_, 8, 1024, wO_d[:, :], gate="ya_dead")
        un = [0]
        dn = [0]
        for hf in range(2):
            for j in range(NFF):
                WS.declare("U%d_%d" % (hf, j), "u", un[0] % 4, RU + (un[0] % 4) * 4096, 8, 256, wU_d[j], gate="merge_done")
                un[0] += 1
            if hf == 0:
                for d in range(8):
                    WS.declare("D0_%d" % d, "d", d % 2, RD + (d % 2) * 6144, 24, 128, wD_d[d], gate="phaseO_done")
            else:
                d1slots = [("d", 0, RD), ("d", 1, RD + 6144), ("e", 0, RH), ("e", 1, RH + 6144),
                           ("f", 0, RU), ("f", 1, RU + 6144), ("d", 0, RD), ("d", 1, RD + 6144)]
                for d in range(8):
                    ring, slot, off = d1slots[d]
                    WS.declare("D1_%d" % d, ring, slot, off, 24, 128, wD_d[d], gate=(None if d < 2 else "up1_done"))

        XL = RX
        XLB0 = RX + 32768
        XLB1 = RY + 24576
        SET0 = RX + 40960
        SET1 = RY
        HALO3 = E_HALO
        bslot = [0]

        def next_bslot():
            s_ = bslot[0] % 4
            bslot[0] += 1
            return s_ * 1024

        def half_tile(col0, wget, wc0, rhs, K, hf, name=""):
            for n in range(2):
                mms = []
                for k in range(K):
                    mms.append((B.ps(col0 + n * 512, 512), wget(k, wc0, wc0 + 128),
                                rhs(k, hf * HW_ + n * 512, hf * HW_ + (n + 1) * 512), k == 0, k == K - 1))
                mm_group(mms, name=name)

        def xlv(j):
            return B.sb(XL + (j % 4) * 8192, F32, S)

        def xlbv(j):
            return B.sb((XLB0 if (j // 2) % 2 == 0 else XLB1) + (j % 2) * 4096, BF16, S)

        def bset(j):
            base = SET0 if j % 2 == 0 else SET1
            return dict(A=[B.sb(base + hf * 12288, F32, HW_) for hf in range(2)],
                        Bm=[B.sb(base + hf * 12288 + 4096, F32, HW_) for hf in range(2)],
                        C=[B.sb(base + hf * 12288 + 8192, F32, HW_) for hf in range(2)])
        lxP = {}

        def lx_pe(j):
            wget = WS.use("B%d" % (j // 2))
            c = j % 2
            assert bslot[0] % 2 == 0
            for hf in range(2):
                col = next_bslot()
                half_tile(col, wget, c * 128, hv, 8, hf, name="lx%d" % j)
                if hf == 0:
                    lxP[j] = col

        def lx_act(j, hf):
            xl = xlv(j)
            a, b = hf * HW_, (hf + 1) * HW_
            act(xl.sl(a, b), B.ps(lxP[j] + a, HW_), AF.Identity, bias=cv(C_LCB + j), scale=cv(C_LCW + 3 * 8 + j), name="lconv_t3")

        def lx_dve(j):
            xl = xlv(j)
            xb = xlbv(j)
            P = B.ps(lxP[j], S)
            for hf in range(2):
                a, b = hf * HW_, (hf + 1) * HW_
                for tap, sh in ((2, 1), (1, 2), (0, 3)):
                    lo = max(a, sh)
                    stt(xl.sl(lo, b), P.sl(lo - sh, b - sh), cv(C_LCW + tap * 8 + j), xl.sl(lo, b), ALU.mult, ALU.add,
                        name="lconv_t%d" % tap)
                if j < 2:
                    act(xb.sl(a, b), xl.sl(a, b), AF.Copy, name="xl_bf")
                else:
                    B.add("dve", lambda e, o=xb.sl(a, b), i=xl.sl(a, b): e.tensor_copy(out=o.ap, in_=i.ap),
                          reads=[xl.sl(a, b)], writes=[xb.sl(a, b)], name="xl_bf")

        def gate_pe_tanh(j):
            hd, c = j // 2, j % 2
            gget = WS.use("G%d" % hd)

            def xlb_rhs(k, a, b):
                return xlbv(2 * hd + k).sl(a, b)
            st = bset(j)
            for hf in range(2):
                ca = next_bslot()
                half_tile(ca, gget, c * 128, xlb_rhs, 2, hf, name="za%d" % j)
                act(st["A"][hf], B.ps(ca, HW_), AF.Tanh, bias=cv(C_HBA + j), scale=0.5, name="r'")
                cx_ = next_bslot()
                half_tile(cx_, gget, 256 + c * 128, xlb_rhs, 2, hf, name="zx%d" % j)
                act(st["C"][hf], B.ps(cx_, HW_), AF.Tanh, bias=cv(C_HBX + j), scale=0.5, name="i'")
            if c == 1:
                WS.done("G%d" % hd)

        def gate_exp_a2(j):
            st = bset(j)
            for hf in range(2):
                act(st["Bm"][hf], st["A"][hf], AF.Exp, bias=cv(C_LS8 + j), scale=cv(C_LS8 + j), name="a2")

        def gate_exp_a(j):
            st = bset(j)
            for hf in range(2):
                act(st["A"][hf], st["A"][hf], AF.Exp, bias=cv(C_LS4 + j), scale=cv(C_LS4 + j), name="a")

        def gate_sqrt(j):
            st = bset(j)
            for hf in range(2):
                act(st["Bm"][hf], st["Bm"][hf], AF.Sqrt, bias=0.25, scale=-0.25, name="mult/2")

        def gate_dve(j):
            st = bset(j)
            xl = xlv(j)
            memset(st["Bm"][0].sl(0, 1), 0.5, eng="dve", name="mult0")
            for hf in range(2):
                a, b = hf * HW_, (hf + 1) * HW_
                stt(st["C"][hf], st["C"][hf], 1.0, xl.sl(a, b), ALU.add, ALU.mult, name="(i'+1)*x")
            for hf in range(2):
                tt(st["C"][hf], st["C"][hf], st["Bm"][hf], ALU.mult, name="u")
            scan(st["Bm"][0], st["A"][0], st["C"][0], 0.0, name="scan0")
            scan(st["Bm"][1], st["A"][1], st["C"][1], st["Bm"][0].sl(HW_ - 1, HW_), name="scan1")

        def ly_pe_act(j):
            hd, c = j // 2, j % 2
            wget = WS.use("B%d" % hd)
            st = bset(j)
            for hf in range(2):
                cy = next_bslot()
                half_tile(cy, wget, 256 + c * 128, hv, 8, hf, name="ly%d" % j)
                act(st["C"][hf], B.ps(cy, HW_), AF.Gelu_apprx_tanh, name="gelu_ly")
            if c == 1:
                WS.done("B%d" % hd)

        def yb_dve(j):
            st = bset(j)
            for hf in range(2):
                a, b = hf * HW_, (hf + 1) * HW_
                tt(ybv(j, a, b), st["Bm"][hf], st["C"][hf], ALU.mult, name="yb", eng=YB_ENG)

        for j in range(-2, 9):
            cur = 0 <= j < 8
            nxt = 0 <= j + 2 < 8
            prv = 0 <= j - 1 < 8
            if cur:
                gate_pe_tanh(j)
            if nxt:
                lx_pe(j + 2)
            if prv:
                gate_dve(j - 1)
            if cur:
                gate_exp_a2(j)
            if nxt:
                lx_act(j + 2, 0)
            if cur:
                gate_exp_a(j)
            if nxt:
                lx_act(j + 2, 1)
            if cur:
                gate_sqrt(j)
            if prv:
                ly_pe_act(j - 1)
            if nxt:
                lx_dve(j + 2)
            if prv:
                yb_dve(j - 1)

        SC_A = RX
        for j in range(8):
            wget = WS.use("A%d" % j)
            sb_ = SC_A + (j % 2) * 16384
            cxq = B.sb(sb_, F32, S)
            pb = B.sb(sb_ + 8192, F32, S)
            c0 = next_slot()
            full_tile(c0, wget, 0, hv, 8, name="cx%d" % j)
            for hf in range(2):
                a, b = hf * HW_, (hf + 1) * HW_
                act(cxq.sl(a, b), B.ps(c0 + a, HW_), AF.Copy, name="cxcopy")
            c1 = next_slot()
            full_tile(c1, wget, 128, hv, 8, name="cc%d" % j)
            for hf in range(2):
                a, b = hf * HW_, (hf + 1) * HW_
                tt(pb.sl(a, b), B.ps(c1 + a, HW_), cxq.sl(a, b), ALU.mult, name="p")
            act(cxq, pb, AF.Identity, scale=cv(C_CSW + 2 * 8 + j), name="conv_t2")
            stt(cxq.sl(1, S), pb.sl(0, S - 1), cv(C_CSW + 1 * 8 + j), cxq.sl(1, S), ALU.mult, ALU.add, name="conv_t1")
            stt(cxq.sl(2, S), pb.sl(0, S - 2), cv(C_CSW + 0 * 8 + j), cxq.sl(2, S), ALU.mult, ALU.add, name="conv_t0")
            c2 = next_slot()
            full_tile(c2, wget, 256, hv, 8, name="cb%d" % j)
            for hf in range(2):
                a, b = hf * HW_, (hf + 1) * HW_
                tt(yav(j, a, b), B.ps(c2 + a, HW_), cxq.sl(a, b), ALU.mult, name="ya")
            WS.done("A%d" % j)

        SC_M = RX
        for j in range(8):
            wget = WS.use("M%d" % j)
            cgc = next_slot()
            full_tile(cgc, wget, 0, hv, 8, name="gc%d" % j)
            S1 = [B.sb(SC_M + (j % 2) * 16384 + hf * 4096, F32, HW_) for hf in range(2)]
            S2 = [B.sb(SC_M + (j % 2) * 16384 + 8192 + hf * 4096, F32, HW_) for hf in range(2)]
            for hf in range(2):
                act(S1[hf], B.ps(cgc + hf * HW_, HW_), AF.Sigmoid, name="sg_conv")
            cba = next_slot()
            full_tile(cba, wget, 128, yav, 8, name="brA%d" % j)
            if j == 7:
                WS.open_gate("ya_dead")
            for hf in range(2):
                tt(S1[hf], B.ps(cba + hf * HW_, HW_), S1[hf], ALU.mult, name="mA")
            cgl = next_slot()
            full_tile(cgl, wget, 256, hv, 8, name="gl%d" % j)
            for hf in range(2):
                act(S2[hf], B.ps(cgl + hf * HW_, HW_), AF.Sigmoid, name="sg_lru")
            cbb = next_slot()
            full_tile(cbb, wget, 384, ybv, 8, name="brB%d" % j)
            for hf in range(2):
                a, b = hf * HW_, (hf + 1) * HW_
                tt(S2[hf], B.ps(cbb + a, HW_), S2[hf], ALU.mult, name="mB")
                tt(mgv(j, a, b), S1[hf], S2[hf], ALU.add, name="merged")
            WS.done("M%d" % j)

        WS.open_gate("merge_done")
        woget = WS.use("O")
        MIXB = RY + 16384
        _, _, sdO, rsO = norm_scratch(RH + 16384)
        for q in range(4):
            dma("sp", B.sb3(RX, F32, 8, S, q * QW, (q + 1) * QW), xT3[:, :, q * QW:(q + 1) * QW], "x%d" % q, name="xreload%d" % q)
        bank_i = [0]
        EARLY_H2 = [None]

        def sq_act(src):
            i = nsq[0] % 4
            nsq[0] += 1
            act(sq_r[i], src, AF.Square, name="sq")
            return sq_r[i]

        def mixv(q, j):
            return B.sb(MIXB + (q % 2) * 16384 + j * 2048, F32, QW)

        ss2_pend = [None]

        def ss2_flush():
            if ss2_pend[0] is not None:
                q_, j_, sqv = ss2_pend[0]
                ss2 = B.ps(7 * 512, QW)
                mm_group([(ss2, ones_r, sqv, j_ == 0, j_ == 7)], name="ss2mm")
                if j_ == 7:
                    a_, b_ = q_ * QW, (q_ + 1) * QW
                    rstd_act(rstd2.sl(a_, b_), sdO[1], ss2, name="rstd2")
                ss2_pend[0] = None

        def ss2_piece(q, j):
            ss2_flush()
            a, b = q * QW, (q + 1) * QW
            ss2_pend[0] = (q, j, sq_act(xv(j, a, b)))

        def mix_stage(q):
            a, b = q * QW, (q + 1) * QW
            ss1 = B.ps(6 * 512, QW)
            pend = None
            for j in range(8):
                pcol = (bank_i[0] % 6) * 512
                bank_i[0] += 1
                P = B.ps(pcol, QW)
                mm_group([(P, woget(k, j * 128, (j + 1) * 128), mgv(k, a, b), k == 0, k == 7) for k in range(8)], name="mix%d" % j)
                if pend is not None:
                    mm_group([(ss1, ones_r, pend[0], pend[1] == 0, False)], name="ss1mm")
                ss2_flush()
                act(mixv(q, j), P, AF.Copy, name="mixcopy")
                pend = (sq_act(P), j)
                if q >= 1:
                    ss2_piece(q - 1, j)
            mm_group([(ss1, ones_r, pend[0], False, True)], name="ss1mm")
            ss2_flush()

        def norm_stage(q):
            a, b = q * QW, (q + 1) * QW
            ss1 = B.ps(6 * 512, QW)
            rstd_act(rsO[0], sdO[0], ss1, name="rstdO")
            for j in range(8):
                stt(mixv(q, j), mixv(q, j), cv(C_G2 + j), rsO[0], ALU.mult, ALU.mult, name="mixn")
                tt(xv(j, a, b), mixv(q, j), xv(j, a, b), ALU.add, name="x1")

        def _early_h2():
            for k in range(8):
                stt(B.sb(RH + k * 2048, BF16, HW_), xv(k, 0, HW_), cv(C_G3 + k), rstd2.sl(0, HW_), ALU.mult, ALU.mult, name="h2")
        EARLY_H2[0] = _early_h2
        for q in range(3):
            mix_stage(q)
            norm_stage(q)
        EARLY_H2[0]()
        mix_stage(3)
        WS.done("O")
        norm_stage(3)

        H2 = RH
        OSB = RH + 16384
        FB = RY
        GV = RY + 49152 + 16384 + 12288
        FSC = GV + 16384
        _, _, sdF, rsF = norm_scratch(FSC, 2, 1)
        assert FSC + 4096 <= RE
        HALO = E_HALO
        bank_u = [0]
        bank_d = [0]
        uidx = [0]
        didx = [0]
        yT3 = yT.rearrange("(k p) t -> p k t", p=128)

        def h2_stage(hf):
            a, b = hf * HW_, (hf + 1) * HW_
            for k in range(8):
                stt(B.sb(H2 + k * 2048, BF16, HW_), xv(k, a, b), cv(C_G3 + k), rstd2.sl(a, b), ALU.mult, ALU.mult, name="h2")

        def h2v(k, c0, c1):
            return B.sb(H2 + k * 2048, BF16, HW_).sl(c0, c1)

        def gvbuf(j, gi):
            return B.sb(GV + ((j % 2) * 2 + gi) * 4096, F32, HW_)

        def up_stage(hf, j):
            wget = WS.use("U%d_%d" % (hf, j))
            Ps = []
            for gi in range(2):
                pcol = ((bank_u[0] % 3) if bank_u[0] < 24 else (bank_u[0] % 4)) * 1024
                bank_u[0] += 1
                mms = []
                for n in range(2):
                    for k in range(8):
                        mms.append((B.ps(pcol + n * 512, 512), wget(k, gi * 128, gi * 128 + 128), h2v(k, n * 512, (n + 1) * 512),
                                    k == 0, k == 7))
                mm_group(mms, name="up%d_%d" % (j, gi))
                Ps.append(B.ps(pcol, HW_))
            WS.done("U%d_%d" % (hf, j))
            for gi in range(2):
                ch = j if gi == 0 else NFF + j
                P, t = Ps[gi], gvbuf(j, gi)
                halo = B.sb(HALO + ch * 8, F32, 2)
                w1, w0 = cv(C_FCW + 1 * 48 + ch), cv(C_FCW + 0 * 48 + ch)
                act(t, P, AF.Identity, bias=cv(C_FCB + ch), scale=cv(C_FCW + 2 * 48 + ch), name="fconv_t2")
                if hf == 0:
                    act(halo, P.sl(HW_ - 2, HW_), AF.Copy, name="halo")
                else:
                    act(t.sl(0, 1), halo.sl(1, 2), AF.Identity, bias=t.sl(0, 1), scale=w1, name="fconv_b1")
                    act(t.sl(0, 1), halo.sl(0, 1), AF.Identity, bias=t.sl(0, 1), scale=w0, name="fconv_b0a")
                    act(t.sl(1, 2), halo.sl(1, 2), AF.Identity, bias=t.sl(1, 2), scale=w0, name="fconv_b0b")
            for gi in range(2):
                ch = j if gi == 0 else NFF + j
                P, t = Ps[gi], gvbuf(j, gi)
                stt(t.sl(1, HW_), P.sl(0, HW_ - 1), cv(C_FCW + 1 * 48 + ch), t.sl(1, HW_), ALU.mult, ALU.add, name="fconv_t1")
                stt(t.sl(2, HW_), P.sl(0, HW_ - 2), cv(C_FCW + 0 * 48 + ch), t.sl(2, HW_), ALU.mult, ALU.add, name="fconv_t0")
            g, v = gvbuf(j, 0), gvbuf(j, 1)
            act(g, g, AF.Gelu_apprx_tanh, name="gelu_f")
            tt(B.sb(FB + j * 2048, BF16, HW_), g, v, ALU.mult, name="f")

        def fv(k, c0, c1):
            return B.sb(FB + k * 2048, BF16, HW_).sl(c0, c1)

        def odv(qq, d):
            return B.sb((OSB if qq == 0 else GV) + d * 2048, F32, QW)

        def down_stage(hf, order, mid_hook=None):
            ss3 = [B.ps(6 * 512, QW), B.ps(7 * 512, QW)]
            pend = None
            KS = 20
            seen_d = {}
            cnt = [0, 0]
            for (d, qq) in order:
                wget = WS.use("D%d_%d" % (hf, d))
                pcol = (bank_d[0] % 6) * 512
                bank_d[0] += 1
                P = B.ps(pcol, QW)
                mm_group([(P, wget(k, 0, 128), fv(k, qq * QW, (qq + 1) * QW), k == 0, False) for k in range(KS)],
                         name="down%d_%d a" % (d, qq))
                mm_group([(P, wget(k, 0, 128), fv(k, qq * QW, (qq + 1) * QW), False, k == NFF - 1) for k in range(KS, NFF)],
                         name="down%d_%d b" % (d, qq))
                if pend is not None:
                    mm_group([(ss3[pend[2]], ones_r, pend[0], pend[3] == 0, pend[3] == 7)], name="ss3mm")
                    if pend[3] == 7 and mid_hook is not None and pend[2] == 0:
                        mid_hook()
                act(odv(qq, d), P, AF.Copy, name="ocopy")
                pend = (sq_act(P), d, qq, cnt[qq])
                cnt[qq] += 1
                seen_d[d] = seen_d.get(d, 0) + 1
                if seen_d[d] == 2:
                    WS.done("D%d_%d" % (hf, d))
            mm_group([(ss3[pend[2]], ones_r, pend[0], pend[3] == 0, pend[3] == 7)], name="ss3mm")

        def post_rstd(qq, rs, tmp):
            ss3 = B.ps((6 + qq) * 512, QW)
            rstd_act(rs, tmp, ss3, name="rstdF")

        def post_apply(hf, qq, rs):
            q = 2 * hf + qq
            ga, gb = q * QW, (q + 1) * QW
            nsplit = 4 if q == 3 else 1
            per = 8 // nsplit
            for d in range(8):
                stt(odv(qq, d), odv(qq, d), cv(C_G4 + d), rs, ALU.mult, ALU.mult, name="on")
                tt(xv(d, ga, gb), odv(qq, d), xv(d, ga, gb), ALU.add, name="out")
                if (d + 1) % per == 0:
                    k0 = d + 1 - per
                    dma("sp", yT3[:, k0:d + 1, ga:gb], B.sb3(RX + k0 * 8192, F32, per, S, ga, gb), "out%d" % q,
                        name="store%d_%d" % (q, k0), out_is_dram=True)
            B.out_chans.append(("out%d" % q, B.chan_val["out%d" % q]))

        ORDER0 = [(d, qq) for d in range(8) for qq in range(2)]
        ORDER1 = [(0, 0), (1, 0), (2, 0), (3, 0), (4, 0), (0, 1), (5, 0), (1, 1), (6, 0), (2, 1), (7, 0), (3, 1),
                  (4, 1), (5, 1), (6, 1), (7, 1)]
        rs_def = B.sb(E_RSTD2, F32, QW)
        rs_tmp = B.sb(E_RSTD2 + 2048, F32, QW)
        for hf in range(2):
            for j in range(NFF):
                up_stage(hf, j)
                if hf == 0 and 2 <= j < 10:
                    ss2_piece(3, j - 2)
                if hf == 0 and j == 10:
                    ss2_flush()
                    WS.open_gate("phaseO_done")
            if hf == 0:
                h2_stage(1)
                down_stage(0, ORDER0)
                post_rstd(1, rsF[0], sdF[0])
                post_rstd(0, rs_def, rs_tmp)
                post_apply(0, 1, rsF[0])
            else:
                WS.open_gate("up1_done")
                post_apply(0, 0, rs_def)

                def _mid():
                    post_rstd(0, rsF[0], sdF[0])
                    post_apply(1, 0, rsF[0])
                down_stage(1, ORDER1, mid_hook=_mid)
                post_rstd(1, rsF[0], sdF[0])
                post_apply(1, 1, rsF[0])

        assert HALO + 48 * 8 <= ARENA_BYTES

        def sem_of(key):
            kind, nm = key
            return sem_eng[nm] if kind == "eng" else chan_sems[nm]

        def emit_engine(e, name):
            for op in B.ops[name]:
                for key, val in op.waits:
                    e.wait_ge(sem_of(key), val)
                inst = op.emit(e)
                if op.is_dma:
                    inst.then_inc(chan_sems[op.chan], 16)
                else:
                    inst.then_inc(sem_eng[name], 1)

        if debug == "deps":
            return B
        block = es.enter_context(nc.Block())

        @block.sync
        def _(e):
            emit_engine(e, "sp")
            for c, v in B.out_chans:
                e.wait_ge(chan_sems[c], v)

        @block.gpsimd
        def _(e):
            emit_engine(e, "pool")

        @block.tensor
        def _(e):
            emit_engine(e, "pe")

        @block.scalar
        def _(e):
            emit_engine(e, "act")

        @block.vector
        def _(e):
            emit_engine(e, "dve")
    return nc


def _tile_w(w):
    K, C = w.shape
    return np.ascontiguousarray(w.reshape(K // 128, 128, C).transpose(1, 0, 2))


def _chunkcols(v, n):
    return np.ascontiguousarray(np.asarray(v, np.float32).reshape(n, 128).T)


def prepare_inputs(x, norm_mix_pre, norm_mix_post, norm_ffn_pre, norm_ffn_post, w_in, conv_short_w,
                   w_conv_branch, lru_conv_w, lru_conv_b, lru_wa, lru_ba, lru_wx, lru_bx, lru_lambda,
                   w_lru_branch, w_out, ffn_w_up, ffn_conv_w, ffn_conv_b, ffn_w_down):
    f = np.float32
    w_in = np.asarray(w_in[0], f)
    cb, cc, cx = w_in[:, 0:1024], w_in[:, 1024:2048], w_in[:, 2048:3072]
    lx, ly = w_in[:, 3072:4096], w_in[:, 4096:5120]
    gconv, glru = w_in[:, 5120:6144], w_in[:, 6144:7168]
    wcb = np.asarray(w_conv_branch[0], f)
    wlb = np.asarray(w_lru_branch[0], f)

    def ch(m, j):
        return m[:, j * 128:(j + 1) * 128]
    wA = np.stack([_tile_w(np.concatenate([ch(cx, j), ch(cc, j), ch(cb, j)], axis=1)).reshape(128, -1) for j in range(8)])
    wB = np.stack([_tile_w(np.concatenate([ch(lx, 2 * h), ch(lx, 2 * h + 1), ch(ly, 2 * h), ch(ly, 2 * h + 1)], axis=1)).reshape(128, -1)
                   for h in range(4)])
    wa = np.asarray(lru_wa[0], f)
    wx = np.asarray(lru_wx[0], f)
    wG = np.stack([_tile_w(np.concatenate([wa[h], wx[h]], axis=1)).reshape(128, -1) for h in range(4)])
    wM = np.stack([_tile_w(np.concatenate([ch(gconv, j), ch(wcb, j), ch(glru, j), ch(wlb, j)], axis=1)).reshape(128, -1)
                   for j in range(8)])
    wO = _tile_w(np.asarray(w_out[0], f)).reshape(128, -1)
    wup = np.asarray(ffn_w_up[0], f)
    gate, val = wup[:, 0:3072], wup[:, 3072:6144]
    wU = np.stack([_tile_w(np.concatenate([ch(gate, j), ch(val, j)], axis=1)).reshape(128, -1) for j in range(24)])
    wdn = np.asarray(ffn_w_down[0], f)
    wD = np.stack([_tile_w(ch(wdn, d)).reshape(128, -1) for d in range(8)])

    cvec = np.zeros((128, NV), f)
    cvec[:, C_G1:C_G1 + 8] = _chunkcols(norm_mix_pre[0], 8)
    cvec[:, C_G2:C_G2 + 8] = _chunkcols(norm_mix_post[0], 8)
    cvec[:, C_G3:C_G3 + 8] = _chunkcols(norm_ffn_pre[0], 8)
    cvec[:, C_G4:C_G4 + 8] = _chunkcols(norm_ffn_post[0], 8)
    for t in range(3):
        cvec[:, C_CSW + t * 8:C_CSW + t * 8 + 8] = _chunkcols(conv_short_w[0][t], 8)
    for t in range(4):
        cvec[:, C_LCW + t * 8:C_LCW + t * 8 + 8] = _chunkcols(lru_conv_w[0][t], 8)
    cvec[:, C_LCB:C_LCB + 8] = _chunkcols(lru_conv_b[0], 8)
    cvec[:, C_LBA:C_LBA + 8] = _chunkcols(np.asarray(lru_ba[0]).reshape(-1), 8)
    cvec[:, C_LBX:C_LBX + 8] = _chunkcols(np.asarray(lru_bx[0]).reshape(-1), 8)
    cvec[:, C_LAM:C_LAM + 8] = _chunkcols(lru_lambda[0], 8)
    for t in range(3):
        cvec[:, C_FCW + t * 48:C_FCW + t * 48 + 48] = _chunkcols(ffn_conv_w[0][t], 48)
    cvec[:, C_FCB:C_FCB + 48] = _chunkcols(ffn_conv_b[0], 48)
    shared = dict(cvec=cvec, wA=wA, wB=wB, wG=wG, wM=wM, wO=wO, wU=wU, wD=wD)
    x = np.asarray(x, f)
    in_maps = []
    for b in range(NCORES):
        m = dict(shared)
        m["xT"] = np.ascontiguousarray(x[b].T)
        in_maps.append(m)
    return in_maps


def kernel(**inputs):
    in_maps = prepare_inputs(**inputs)
    nc = build_program()
    res = run_bass_kernel_spmd(nc, in_maps, core_ids=list(range(NCORES)))
    out = np.stack([np.ascontiguousarray(res.results[b]["yT"].T) for b in range(NCORES)], axis=0)
    return out.astype(np.float32)
```

```python
import numpy as np
import concourse.bass as bass
import concourse.mybir as mybir
from concourse.bass_utils import run_bass_kernel_spmd

F32 = mybir.dt.float32
F32R = mybir.dt.float32r
BF16 = mybir.dt.bfloat16
AF = mybir.ActivationFunctionType
ALU = mybir.AluOpType

D = 1024
S = 2048
NCH = 8
NFF = 24
NCORES = 8
EPS = 1e-6
QW = 512
HW_ = 1024

C_G1, C_G2, C_G3, C_G4 = 0, 8, 16, 24
C_CSW = 32
C_LCW = 56
C_LCB = 88
C_LBA = 96
C_LBX = 104
C_LAM = 112
C_FCW = 120
C_FCB = 264
C_LS8 = 312
C_LS4 = 320
C_HBA = 328
C_HBX = 336
C_TMP = 344
NV = 360
assert NV * 4 <= 1536
YB_ENG = "dve"

RX = 0
RH = 65536
RY = 98304
RE = 196608
E_CV = RE
E_ONES = E_CV + 1536
E_RSTD2 = E_ONES + 256
E_HALO = E_RSTD2 + 8192
E_SQ = E_HALO + 512
E_END = E_SQ + 4096
ARENA_BYTES = E_END
BLK = 256


class View:
    __slots__ = ("space", "ap", "ranges", "esz", "lo")

    def __init__(self, space, ap, ranges, esz, lo):
        self.space, self.ap, self.ranges, self.esz, self.lo = space, ap, ranges, esz, lo

    def sl(self, a, b):
        lo = self.lo + a * self.esz
        return View(self.space, self.ap[:, a:b], [(lo, self.lo + b * self.esz)], self.esz, lo)


class Op:
    __slots__ = ("eng", "emit", "reads", "writes", "count", "waits", "chan", "dma_val", "is_dma", "name", "dbg")


class Builder:
    def __init__(self, nc, arena, psum, rarena):
        self.nc = nc
        self.arena = arena
        self.psum = psum
        self.rarena = rarena
        self.ops = {e: [] for e in ("pe", "act", "dve", "pool", "sp")}
        self.ncompute = {e: 0 for e in ("pe", "act", "dve", "pool", "sp")}
        self.lastw = {"sb": {}, "ps": {}, "sr": {}, "tk": {}}
        self.readers = {"sb": {}, "ps": {}, "sr": {}, "tk": {}}
        self.seen = {e: {} for e in ("pe", "act", "dve", "pool", "sp")}
        self.chan_val = {}
        self.out_chans = []

    def sb(self, off, dtype, n):
        esz = 2 if dtype == BF16 else 4
        assert off % 4 == 0 and (n * esz) % 4 == 0
        a = self.arena[:, off // 4:(off + n * esz) // 4]
        if dtype != F32:
            a = a.bitcast(dtype)
        return View("sb", a, [(off, off + n * esz)], esz, off)

    def sb3(self, off, dtype, k, n, c0, c1, kstride=None):
        esz = 2 if dtype == BF16 else 4
        a = self.arena[:, off // 4:(off + k * n * esz) // 4]
        if dtype != F32:
            a = a.bitcast(dtype)
        a = a.rearrange("p (k n) -> p k n", k=k)[:, :, c0:c1]
        rg = [(off + (i * n + c0) * esz, off + (i * n + c1) * esz) for i in range(k)]
        return View("sb", a, rg, esz, off)

    def sr(self, e0, n):
        return View("sr", self.rarena[:, e0:e0 + n], [(e0 * 4, (e0 + n) * 4)], 4, e0 * 4)

    def tok(self, i):
        return View("tk", None, [(i * BLK, (i + 1) * BLK)], 4, i * BLK)

    def ps(self, col0, n):
        return View("ps", self.psum[:, col0:col0 + n], [(col0 * 4, (col0 + n) * 4)], 4, col0 * 4)

    @staticmethod
    def _blocks(v):
        for lo, hi in v.ranges:
            for b in range(lo // BLK, (hi - 1) // BLK + 1):
                yield b

    def add(self, eng, emit, reads=(), writes=(), chan=None, name=""):
        op = Op()
        op.eng, op.emit, op.name = eng, emit, name
        op.is_dma = chan is not None
        op.chan = chan
        deps = {}
        for v in reads:
            lw = self.lastw[v.space]
            for b in self._blocks(v):
                w = lw.get(b)
                if w is not None:
                    deps[id(w)] = w
        for v in writes:
            lw = self.lastw[v.space]
            rd = self.readers[v.space]
            for b in self._blocks(v):
                w = lw.get(b)
                if w is not None:
                    deps[id(w)] = w
                r = rd.get(b)
                if r:
                    for o in r.values():
                        deps[id(o)] = o
        for v in reads:
            rd = self.readers[v.space]
            for b in self._blocks(v):
                r = rd.get(b)
                if r is None:
                    r = rd[b] = {}
                key = eng if not op.is_dma else ("dma", id(op))
                r[key] = op
        for v in writes:
            lw = self.lastw[v.space]
            rd = self.readers[v.space]
            for b in self._blocks(v):
                lw[b] = op
                rd[b] = {}
        if op.is_dma:
            self.chan_val[chan] = self.chan_val.get(chan, 0) + 16
            op.dma_val = self.chan_val[chan]
            op.count = None
        else:
            self.ncompute[eng] += 1
            op.count = self.ncompute[eng]
        waits = []
        seen = self.seen[eng]
        for d in deps.values():
            if d is op:
                continue
            if d.is_dma:
                key, val = ("chan", d.chan), d.dma_val
            else:
                if d.eng == "pe" and eng == "pe" and not op.is_dma:
                    continue
                key, val = ("eng", d.eng), d.count
            if seen.get(key, 0) < val:
                seen[key] = val
                waits.append((key, val))
        best = {}
        for key, val in waits:
            if best.get(key, 0) < val:
                best[key] = val
        op.waits = list(best.items())
        op.dbg = [(d.name, d.eng, d.count if not d.is_dma else d.dma_val) for d in deps.values()]
        self.ops[eng].append(op)
        return op


def build_program(debug=None):
    nc = bass.Bass("TRN2", target_bir_lowering=False)
    xT = nc.dram_tensor("xT", [D, S], F32, kind="ExternalInput").ap()
    cvec_d = nc.dram_tensor("cvec", [128, NV], F32, kind="ExternalInput").ap()
    wA_d = nc.dram_tensor("wA", [8, 128, 8 * 384], F32, kind="ExternalInput").ap()
    wB_d = nc.dram_tensor("wB", [4, 128, 8 * 512], F32, kind="ExternalInput").ap()
    wG_d = nc.dram_tensor("wG", [4, 128, 2 * 512], F32, kind="ExternalInput").ap()
    wM_d = nc.dram_tensor("wM", [8, 128, 8 * 512], F32, kind="ExternalInput").ap()
    wO_d = nc.dram_tensor("wO", [128, 8 * 1024], F32, kind="ExternalInput").ap()
    wU_d = nc.dram_tensor("wU", [24, 128, 8 * 256], F32, kind="ExternalInput").ap()
    wD_d = nc.dram_tensor("wD", [8, 128, 24 * 128], F32, kind="ExternalInput").ap()
    yT = nc.dram_tensor("yT", [D, S], F32, kind="ExternalOutput").ap()
    dbg_d = None
    if debug:
        dbg_d = nc.dram_tensor("dbg", [128, 8 * 2048], F32, kind="ExternalOutput").ap()

    from contextlib import ExitStack
    with ExitStack() as es:
        arena = es.enter_context(nc.sbuf_tensor("arena", [128, ARENA_BYTES // 4], F32))
        psum = es.enter_context(nc.psum_tensor("psum", [128, 4096], F32))
        B = Builder(nc, arena, psum, None)
        sem_eng = {e: es.enter_context(nc.semaphore("sem_" + e)) for e in ("pe", "act", "dve", "pool")}
        chan_sems = {}

        def chan_sem(c):
            if c not in chan_sems:
                chan_sems[c] = es.enter_context(nc.semaphore("ch_" + c))
            return chan_sems[c]

        def dma(queue, out_view, in_ap, chan, name="", out_is_dram=False, reads=(), writes=()):
            chan_sem(chan)
            if out_is_dram:
                return B.add(queue, lambda e, o=out_view, i=in_ap: e.dma_start(out=o, in_=i.ap),
                             reads=[in_ap], writes=[], chan=chan, name=name)
            return B.add(queue, lambda e, o=out_view, i=in_ap: e.dma_start(out=o.ap, in_=i),
                         reads=list(reads), writes=[out_view] + list(writes), chan=chan, name=name)

        def act(out, in_, func, bias=None, scale=1.0, name="", extra_reads=()):
            kw = {}
            rd = [in_] + list(extra_reads)
            if bias is not None:
                if isinstance(bias, View):
                    kw["bias"] = bias.ap
                    rd.append(bias)
                else:
                    kw["bias"] = bias
            if isinstance(scale, View):
                rd.append(scale)
                sc = scale.ap
            else:
                sc = scale
            return B.add("act", lambda e, o=out, i=in_, f=func, sc=sc, kw=kw: e.activation(out=o.ap, in_=i.ap, func=f, scale=sc, **kw),
                         reads=rd, writes=[out], name=name)

        def tt(out, in0, in1, op, name="", eng="dve"):
            return B.add(eng, lambda e, o=out, a=in0, b=in1, op=op: e.tensor_tensor(out=o.ap, in0=a.ap, in1=b.ap, op=op),
                         reads=[in0, in1], writes=[out], name=name)

        def stt(out, in0, scalar, in1, op0, op1, name=""):
            rd = [in0, in1]
            if isinstance(scalar, View):
                rd.append(scalar)
                sc = scalar.ap
            else:
                sc = scalar
            return B.add("dve", lambda e, o=out, a=in0, sc=sc, b=in1, op0=op0, op1=op1: e.scalar_tensor_tensor(
                out=o.ap, in0=a.ap, scalar=sc, in1=b.ap, op0=op0, op1=op1), reads=rd, writes=[out], name=name)

        def ts(out, in0, s1, s2, op0, op1=None, name="", eng="dve"):
            rd = [in0]
            a1 = s1.ap if isinstance(s1, View) else s1
            a2 = s2.ap if isinstance(s2, View) else s2
            if isinstance(s1, View):
                rd.append(s1)
            if isinstance(s2, View):
                rd.append(s2)
            if op1 is None:
                return B.add(eng, lambda e, o=out, a=in0, a1=a1, op0=op0: e.tensor_scalar(
                    out=o.ap, in0=a.ap, scalar1=a1, scalar2=None, op0=op0), reads=rd, writes=[out], name=name)
            return B.add(eng, lambda e, o=out, a=in0, a1=a1, a2=a2, op0=op0, op1=op1: e.tensor_scalar(
                out=o.ap, in0=a.ap, scalar1=a1, scalar2=a2, op0=op0, op1=op1), reads=rd, writes=[out], name=name)

        def rstd_act(out, tmp, ss, name=""):
            act(tmp, ss, AF.Ln, bias=EPS, scale=1.0 / D, name=name + "_ln")
            act(out, tmp, AF.Exp, scale=-0.5, name=name)

        def recip(out, in_, name=""):
            return B.add("dve", lambda e, o=out, i=in_: e.reciprocal(out=o.ap, in_=i.ap), reads=[in_], writes=[out], name=name)

        def memset(out, val, eng="dve", name=""):
            return B.add(eng, lambda e, o=out, v=val: e.memset(o.ap, v), reads=[], writes=[out], name=name)

        def scan(out, d0, d1, init, name=""):
            rd = [d0, d1]
            if isinstance(init, View):
                rd.append(init)
                ini = init.ap
            else:
                ini = init
            return B.add("dve", lambda e, o=out, a=d0, b=d1, ini=ini: e.tensor_tensor_scan(
                out=o.ap, data0=a.ap, data1=b.ap, initial=ini, op0=ALU.mult, op1=ALU.add), reads=rd, writes=[out], name=name)

        def mm_group(mms, name=""):
            reads, writes = [], []
            for (o, l, r, st, sp) in mms:
                reads.append(l)
                reads.append(r)
                writes.append(o)

            def emit(e, mms=mms):
                inst = None
                for (o, l, r, st, sp) in mms:
                    inst = e.matmul(o.ap, l.ap, r.ap, start=st, stop=sp)
                return inst
            return B.add("pe", emit, reads=reads, writes=writes, name=name)

        def cv(col, n=1):
            return B.sb(E_CV + col * 4, F32, n)
        ones_r = B.sb(E_ONES, BF16, 128)
        sq_r = [B.sb(E_SQ + i * 1024, BF16, QW) for i in range(4)]
        rstd2 = B.sb(E_RSTD2, F32, S)

        def xv(k, c0=0, c1=S):
            return B.sb(RX + k * 8192, F32, S).sl(c0, c1)

        def hv(k, c0=0, c1=S):
            return B.sb(RH + k * 4096, BF16, S).sl(c0, c1)

        def yav(k, c0=0, c1=S):
            return B.sb(RY + k * 4096, BF16, S).sl(c0, c1)

        def ybv(k, c0=0, c1=S):
            return B.sb(RY + 32768 + k * 4096, BF16, S).sl(c0, c1)

        def mgv(k, c0=0, c1=S):
            return B.sb(RY + 65536 + k * 4096, BF16, S).sl(c0, c1)

        def bcast(v, k):
            n = v.ap.shape[-1]
            return View(v.space, v.ap.unsqueeze(1).broadcast_to([128, k, n]), v.ranges, v.esz, v.lo)

        def wtile(off, kk, cols):
            full = B.sb(off, BF16, kk * cols)

            def get(k, c0, c1):
                return full.sl(k * cols + c0, k * cols + c1)
            return full, get

        dma("sp", cv(0, NV), cvec_d[:, :], "cv", name="cvec")
        memset(ones_r, 1.0, eng="dve", name="ones")
        act(cv(C_TMP, 8), cv(C_LAM, 8), AF.Exp, scale=-1.0, name="exp(-lam)")
        act(cv(C_TMP + 8, 8), cv(C_TMP, 8), AF.Ln, bias=1.0, name="ln1p")
        ts(cv(C_LS8, 8), cv(C_TMP + 8, 8), -8.0, None, ALU.mult, name="ls8")
        ts(cv(C_LS4, 8), cv(C_TMP + 8, 8), -4.0, None, ALU.mult, name="ls4")
        ts(cv(C_HBA, 16), cv(C_LBA, 16), 0.5, None, ALU.mult, name="half_biases")

        xT3 = xT.rearrange("(k p) t -> p k t", p=128)
        for q in range(4):
            if q == 0:
                for g in range(2):
                    dma("sp", B.sb3(RX + g * 4 * 8192, F32, 4, S, 0, QW), xT3[:, g * 4:(g + 1) * 4, 0:QW], "x0%s" % "ab"[g],
                        name="xload0%s" % "ab"[g], writes=[B.tok(0)])
                continue
            dma("sp", B.sb3(RX, F32, 8, S, q * QW, (q + 1) * QW), xT3[:, :, q * QW:(q + 1) * QW], "x%d" % q, name="xload%d" % q,
                writes=[B.tok(q)])

        def norm_scratch(base, nsq=4, nsd=2):
            sq = sqr = sq_r
            o = base
            sd = [B.sb(o + i * 2048, F32, QW) for i in range(nsd)]
            o += nsd * 2048
            rs = [B.sb(o + i * 2048, F32, QW) for i in range(nsd)]
            return sq, sqr, sd, rs

        sq0, sq0r, sd0, rs0 = norm_scratch(RY)
        nsq = [0]

        def sumsq_chunk(src_view, ss_ps, first, last, sqs, sqrs):
            i = nsq[0] % len(sqs)
            nsq[0] += 1
            act(sqrs[i], src_view, AF.Square, name="sq")
            mm_group([(ss_ps, ones_r, sqrs[i], first, last)], name="ssmm")

        for q in range(4):
            ss = B.ps((4 + q) * 512, QW)
            for k in range(8):
                sumsq_chunk(xv(k, q * QW, (q + 1) * QW), ss, k == 0, k == 7, sq0, sq0r)
            rstd_act(rs0[q % 2], sd0[q % 2], ss, name="rstd1")
            for k in range(8):
                stt(hv(k, q * QW, (q + 1) * QW), xv(k, q * QW, (q + 1) * QW), cv(C_G1 + k), rs0[q % 2],
                    ALU.mult, ALU.mult, name="h")

        slot_i = [0]

        def next_slot():
            s = slot_i[0] % 2
            slot_i[0] += 1
            return s * 2048

        def full_tile(col0, wget, wc0, rhs, K, name=""):
            for hf in range(2):
                mms = []
                for n in (2 * hf, 2 * hf + 1):
                    for k in range(K):
                        mms.append((B.ps(col0 + n * 512, 512), wget(k, wc0, wc0 + 128), rhs(k, n * 512, (n + 1) * 512),
                                    k == 0, k == K - 1))
                mm_group(mms, name=name)

        class WStream:
            def __init__(self):
                self.tiles = []
                self.issued = 0
                self.slot_busy = {}
                self.index = {}
                self.gates = set()
                self.last_use = 0

            def declare(self, key, ring, slot, off, kk, cols, src, gate=None):
                self.index[key] = len(self.tiles)
                self.tiles.append(dict(key=key, off=off, kk=kk, cols=cols, src=src, chan="w%s%d" % (ring, slot),
                                       slotkey=(ring, slot), gate=gate))

            def open_gate(self, g):
                self.gates.add(g)
                self.pump(upto=self.last_use + LOOKAHEAD)

            def pump(self, upto=None):
                while self.issued < len(self.tiles):
                    t = self.tiles[self.issued]
                    if t["slotkey"] in self.slot_busy:
                        break
                    if t["gate"] is not None and t["gate"] not in self.gates:
                        break
                    if upto is not None and self.issued > upto:
                        break
                    full, get = wtile(t["off"], t["kk"], t["cols"])
                    dma("pool", full, t["src"], t["chan"], name="w_" + t["key"],
                        reads=([B.tok(0)] if t["key"] == "B0" else [B.tok(3)] if t["key"] == "B1" else []))
                    t["get"] = get
                    self.slot_busy[t["slotkey"]] = self.issued
                    self.issued += 1

            def use(self, key):
                n = self.index[key]
                self.last_use = max(self.last_use, n)
                if n >= self.issued:
                    self.pump(upto=n)
                assert n < self.issued, "weight tile %s cannot be issued (slot busy)" % key
                self.pump(upto=n + LOOKAHEAD)
                return self.tiles[n]["get"]

            def done(self, key):
                n = self.index[key]
                t = self.tiles[n]
                assert self.slot_busy.get(t["slotkey"]) == n
                del self.slot_busy[t["slotkey"]]
                self.pump(upto=n + LOOKAHEAD)

        LOOKAHEAD = 3
        WS = WStream()
        R1 = RY + 65536
        RG = RY + 65536 + 24576
        RM = RX + 40960
        RO_ = RY
        RU = RY + 49152
        RD = RY + 49152 + 16384
        r1n = [0]

        def decl_r1(key, kk, cols, src):
            WS.declare(key, "a", r1n[0] % 3, R1 + (r1n[0] % 3) * 8192, kk, cols, src)
            r1n[0] += 1
        decl_r1("B0", 8, 512, wB_d[0])
        decl_r1("B1", 8, 512, wB_d[1])
        WS.declare("G0", "g", 0, RG, 2, 512, wG_d[0])
        decl_r1("B2", 8, 512, wB_d[2])
        WS.declare("G1", "g", 1, RG + 2048, 2, 512, wG_d[1])
        decl_r1("B3", 8, 512, wB_d[3])
        WS.declare("G2", "g", 0, RG, 2, 512, wG_d[2])
        WS.declare("G3", "g", 1, RG + 2048, 2, 512, wG_d[3])
        for j in range(8):
            decl_r1("A%d" % j, 8, 384, wA_d[j])
        for j in range(8):
            WS.declare("M%d" % j, "m", j % 3, RM + (j % 3) * 8192, 8, 512, wM_d[j])
        WS.declare("O", "o", 0, RO_, 8, 1024, wO_d[:, :], gate="ya_dead")
        un = [0]
        dn = [0]
        for hf in range(2):
            for j in range(NFF):
                WS.declare("U%d_%d" % (hf, j), "u", un[0] % 4, RU + (un[0] % 4) * 4096, 8, 256, wU_d[j], gate="merge_done")
                un[0] += 1
            if hf == 0:
                for d in range(8):
                    WS.declare("D0_%d" % d, "d", d % 2, RD + (d % 2) * 6144, 24, 128, wD_d[d], gate="phaseO_done")
            else:
                d1slots = [("d", 0, RD), ("d", 1, RD + 6144), ("e", 0, RH), ("e", 1, RH + 6144),
                           ("f", 0, RU), ("f", 1, RU + 6144), ("d", 0, RD), ("d", 1, RD + 6144)]
                for d in range(8):
                    ring, slot, off = d1slots[d]
                    WS.declare("D1_%d" % d, ring, slot, off, 24, 128, wD_d[d], gate=(None if d < 2 else "up1_done"))

        XL = RX
        XLB0 = RX + 32768
        XLB1 = RY + 24576
        SET0 = RX + 40960
        SET1 = RY
        HALO3 = E_HALO
        bslot = [0]

        def next_bslot():
            s_ = bslot[0] % 4
            bslot[0] += 1
            return s_ * 1024

        def half_tile(col0, wget, wc0, rhs, K, hf, name=""):
            for n in range(2):
                mms = []
                for k in range(K):
                    mms.append((B.ps(col0 + n * 512, 512), wget(k, wc0, wc0 + 128),
                                rhs(k, hf * HW_ + n * 512, hf * HW_ + (n + 1) * 512), k == 0, k == K - 1))
                mm_group(mms, name=name)

        def xlv(j):
            return B.sb(XL + (j % 4) * 8192, F32, S)

        def xlbv(j):
            return B.sb((XLB0 if (j // 2) % 2 == 0 else XLB1) + (j % 2) * 4096, BF16, S)

        def bset(j):
            base = SET0 if j % 2 == 0 else SET1
            return dict(A=[B.sb(base + hf * 12288, F32, HW_) for hf in range(2)],
                        Bm=[B.sb(base + hf * 12288 + 4096, F32, HW_) for hf in range(2)],
                        C=[B.sb(base + hf * 12288 + 8192, F32, HW_) for hf in range(2)])
        lxP = {}

        def lx_pe(j):
            wget = WS.use("B%d" % (j // 2))
            c = j % 2
            assert bslot[0] % 2 == 0
            for hf in range(2):
                col = next_bslot()
                half_tile(col, wget, c * 128, hv, 8, hf, name="lx%d" % j)
                if hf == 0:
                    lxP[j] = col

        def lx_act(j, hf):
            xl = xlv(j)
            a, b = hf * HW_, (hf + 1) * HW_
            act(xl.sl(a, b), B.ps(lxP[j] + a, HW_), AF.Identity, bias=cv(C_LCB + j), scale=cv(C_LCW + 3 * 8 + j), name="lconv_t3")

        def lx_dve(j):
            xl = xlv(j)
            xb = xlbv(j)
            P = B.ps(lxP[j], S)
            for hf in range(2):
                a, b = hf * HW_, (hf + 1) * HW_
                for tap, sh in ((2, 1), (1, 2), (0, 3)):
                    lo = max(a, sh)
                    stt(xl.sl(lo, b), P.sl(lo - sh, b - sh), cv(C_LCW + tap * 8 + j), xl.sl(lo, b), ALU.mult, ALU.add,
                        name="lconv_t%d" % tap)
                B.add("dve", lambda e, o=xb.sl(a, b), i=xl.sl(a, b): e.tensor_copy(out=o.ap, in_=i.ap),
                      reads=[xl.sl(a, b)], writes=[xb.sl(a, b)], name="xl_bf")

        def gate_pe_tanh(j):
            hd, c = j // 2, j % 2
            gget = WS.use("G%d" % hd)

            def xlb_rhs(k, a, b):
                return xlbv(2 * hd + k).sl(a, b)
            st = bset(j)
            for hf in range(2):
                ca = next_bslot()
                half_tile(ca, gget, c * 128, xlb_rhs, 2, hf, name="za%d" % j)
                act(st["A"][hf], B.ps(ca, HW_), AF.Tanh, bias=cv(C_HBA + j), scale=0.5, name="r'")
                cx_ = next_bslot()
                half_tile(cx_, gget, 256 + c * 128, xlb_rhs, 2, hf, name="zx%d" % j)
                act(st["C"][hf], B.ps(cx_, HW_), AF.Tanh, bias=cv(C_HBX + j), scale=0.5, name="i'")
            if c == 1:
                WS.done("G%d" % hd)

        def gate_exp_a2(j):
            st = bset(j)
            for hf in range(2):
                act(st["Bm"][hf], st["A"][hf], AF.Exp, bias=cv(C_LS8 + j), scale=cv(C_LS8 + j), name="a2")

        def gate_exp_a(j):
            st = bset(j)
            for hf in range(2):
                act(st["A"][hf], st["A"][hf], AF.Exp, bias=cv(C_LS4 + j), scale=cv(C_LS4 + j), name="a")

        def gate_sqrt(j):
            st = bset(j)
            for hf in range(2):
                act(st["Bm"][hf], st["Bm"][hf], AF.Sqrt, bias=0.25, scale=-0.25, name="mult/2")

        def gate_dve(j):
            st = bset(j)
            xl = xlv(j)
            memset(st["Bm"][0].sl(0, 1), 0.5, eng="dve", name="mult0")
            for hf in range(2):
                a, b = hf * HW_, (hf + 1) * HW_
                stt(st["C"][hf], st["C"][hf], 1.0, xl.sl(a, b), ALU.add, ALU.mult, name="(i'+1)*x")
            for hf in range(2):
                tt(st["C"][hf], st["C"][hf], st["Bm"][hf], ALU.mult, name="u")
            scan(st["Bm"][0], st["A"][0], st["C"][0], 0.0, name="scan0")
            scan(st["Bm"][1], st["A"][1], st["C"][1], st["Bm"][0].sl(HW_ - 1, HW_), name="scan1")

        def ly_pe_act(j):
            hd, c = j // 2, j % 2
            wget = WS.use("B%d" % hd)
            st = bset(j)
            for hf in range(2):
                cy = next_bslot()
                half_tile(cy, wget, 256 + c * 128, hv, 8, hf, name="ly%d" % j)
                act(st["C"][hf], B.ps(cy, HW_), AF.Gelu_apprx_tanh, name="gelu_ly")
            if c == 1:
                WS.done("B%d" % hd)

        def yb_dve(j):
            st = bset(j)
            for hf in range(2):
                a, b = hf * HW_, (hf + 1) * HW_
                tt(ybv(j, a, b), st["Bm"][hf], st["C"][hf], ALU.mult, name="yb", eng=YB_ENG)

        for j in range(-2, 9):
            cur = 0 <= j < 8
            nxt = 0 <= j + 2 < 8
            prv = 0 <= j - 1 < 8
            if cur:
                gate_pe_tanh(j)
            if nxt:
                lx_pe(j + 2)
            if prv:
                gate_dve(j - 1)
            if cur:
                gate_exp_a2(j)
            if nxt:
                lx_act(j + 2, 0)
            if cur:
                gate_exp_a(j)
            if nxt:
                lx_act(j + 2, 1)
            if cur:
                gate_sqrt(j)
            if prv:
                ly_pe_act(j - 1)
            if nxt:
                lx_dve(j + 2)
            if prv:
                yb_dve(j - 1)

        SC_A = RX
        for j in range(8):
            wget = WS.use("A%d" % j)
            sb_ = SC_A + (j % 2) * 16384
            cxq = B.sb(sb_, F32, S)
            pb = B.sb(sb_ + 8192, F32, S)
            c0 = next_slot()
            full_tile(c0, wget, 0, hv, 8, name="cx%d" % j)
            for hf in range(2):
                a, b = hf * HW_, (hf + 1) * HW_
                act(cxq.sl(a, b), B.ps(c0 + a, HW_), AF.Copy, name="cxcopy")
            c1 = next_slot()
            full_tile(c1, wget, 128, hv, 8, name="cc%d" % j)
            for hf in range(2):
                a, b = hf * HW_, (hf + 1) * HW_
                tt(pb.sl(a, b), B.ps(c1 + a, HW_), cxq.sl(a, b), ALU.mult, name="p")
            act(cxq, pb, AF.Identity, scale=cv(C_CSW + 2 * 8 + j), name="conv_t2")
            stt(cxq.sl(1, S), pb.sl(0, S - 1), cv(C_CSW + 1 * 8 + j), cxq.sl(1, S), ALU.mult, ALU.add, name="conv_t1")
            stt(cxq.sl(2, S), pb.sl(0, S - 2), cv(C_CSW + 0 * 8 + j), cxq.sl(2, S), ALU.mult, ALU.add, name="conv_t0")
            c2 = next_slot()
            full_tile(c2, wget, 256, hv, 8, name="cb%d" % j)
            for hf in range(2):
                a, b = hf * HW_, (hf + 1) * HW_
                tt(yav(j, a, b), B.ps(c2 + a, HW_), cxq.sl(a, b), ALU.mult, name="ya")
            WS.done("A%d" % j)

        SC_M = RX
        for j in range(8):
            wget = WS.use("M%d" % j)
            cgc = next_slot()
            full_tile(cgc, wget, 0, hv, 8, name="gc%d" % j)
            S1 = [B.sb(SC_M + (j % 2) * 16384 + hf * 4096, F32, HW_) for hf in range(2)]
            S2 = [B.sb(SC_M + (j % 2) * 16384 + 8192 + hf * 4096, F32, HW_) for hf in range(2)]
            for hf in range(2):
                act(S1[hf], B.ps(cgc + hf * HW_, HW_), AF.Sigmoid, name="sg_conv")
            cba = next_slot()
            full_tile(cba, wget, 128, yav, 8, name="brA%d" % j)
            if j == 7:
                WS.open_gate("ya_dead")
            for hf in range(2):
                tt(S1[hf], B.ps(cba + hf * HW_, HW_), S1[hf], ALU.mult, name="mA")
            cgl = next_slot()
            full_tile(cgl, wget, 256, hv, 8, name="gl%d" % j)
            for hf in range(2):
                act(S2[hf], B.ps(cgl + hf * HW_, HW_), AF.Sigmoid, name="sg_lru")
            cbb = next_slot()
            full_tile(cbb, wget, 384, ybv, 8, name="brB%d" % j)
            for hf in range(2):
                a, b = hf * HW_, (hf + 1) * HW_
                tt(S2[hf], B.ps(cbb + a, HW_), S2[hf], ALU.mult, name="mB")
                tt(mgv(j, a, b), S1[hf], S2[hf], ALU.add, name="merged")
            WS.done("M%d" % j)

        WS.open_gate("merge_done")
        woget = WS.use("O")
        MIXB = RY + 16384
        _, _, sdO, rsO = norm_scratch(RH + 16384)
        for q in range(4):
            dma("sp", B.sb3(RX, F32, 8, S, q * QW, (q + 1) * QW), xT3[:, :, q * QW:(q + 1) * QW], "x%d" % q, name="xreload%d" % q)
        bank_i = [0]
        EARLY_H2 = [None]

        def sq_act(src):
            i = nsq[0] % 4
            nsq[0] += 1
            act(sq_r[i], src, AF.Square, name="sq")
            return sq_r[i]

        def mixv(q, j):
            return B.sb(MIXB + (q % 2) * 16384 + j * 2048, F32, QW)

        ss2_pend = [None]

        def ss2_flush():
            if ss2_pend[0] is not None:
                q_, j_, sqv = ss2_pend[0]
                ss2 = B.ps(7 * 512, QW)
                mm_group([(ss2, ones_r, sqv, j_ == 0, j_ == 7)], name="ss2mm")
                if j_ == 7:
                    a_, b_ = q_ * QW, (q_ + 1) * QW
                    rstd_act(rstd2.sl(a_, b_), sdO[1], ss2, name="rstd2")
                ss2_pend[0] = None

        def ss2_piece(q, j):
            ss2_flush()
            a, b = q * QW, (q + 1) * QW
            ss2_pend[0] = (q, j, sq_act(xv(j, a, b)))

        def mix_stage(q):
            a, b = q * QW, (q + 1) * QW
            ss1 = B.ps(6 * 512, QW)
            pend = None
            for j in range(8):
                pcol = (bank_i[0] % 6) * 512
                bank_i[0] += 1
                P = B.ps(pcol, QW)
                mm_group([(P, woget(k, j * 128, (j + 1) * 128), mgv(k, a, b), k == 0, k == 7) for k in range(8)], name="mix%d" % j)
                if pend is not None:
                    mm_group([(ss1, ones_r, pend[0], pend[1] == 0, False)], name="ss1mm")
                ss2_flush()
                act(mixv(q, j), P, AF.Copy, name="mixcopy")
                pend = (sq_act(P), j)
                if q >= 1:
                    ss2_piece(q - 1, j)
            mm_group([(ss1, ones_r, pend[0], False, True)], name="ss1mm")
            ss2_flush()

        def norm_stage(q):
            a, b = q * QW, (q + 1) * QW
            ss1 = B.ps(6 * 512, QW)
            rstd_act(rsO[0], sdO[0], ss1, name="rstdO")
            for j in range(8):
                stt(mixv(q, j), mixv(q, j), cv(C_G2 + j), rsO[0], ALU.mult, ALU.mult, name="mixn")
                tt(xv(j, a, b), mixv(q, j), xv(j, a, b), ALU.add, name="x1")

        def _early_h2():
            for k in range(8):
                stt(B.sb(RH + k * 2048, BF16, HW_), xv(k, 0, HW_), cv(C_G3 + k), rstd2.sl(0, HW_), ALU.mult, ALU.mult, name="h2")
        EARLY_H2[0] = _early_h2
        for q in range(3):
            mix_stage(q)
            norm_stage(q)
        EARLY_H2[0]()
        mix_stage(3)
        WS.done("O")
        norm_stage(3)

        H2 = RH
        OSB = RH + 16384
        FB = RY
        GV = RY + 49152 + 16384 + 12288
        FSC = GV + 16384
        _, _, sdF, rsF = norm_scratch(FSC, 2, 1)
        assert FSC + 4096 <= RE
        HALO = E_HALO
        bank_u = [0]
        bank_d = [0]
        uidx = [0]
        didx = [0]
        yT3 = yT.rearrange("(k p) t -> p k t", p=128)

        def h2_stage(hf):
            a, b = hf * HW_, (hf + 1) * HW_
            for k in range(8):
                stt(B.sb(H2 + k * 2048, BF16, HW_), xv(k, a, b), cv(C_G3 + k), rstd2.sl(a, b), ALU.mult, ALU.mult, name="h2")

        def h2v(k, c0, c1):
            return B.sb(H2 + k * 2048, BF16, HW_).sl(c0, c1)

        def gvbuf(j, gi):
            return B.sb(GV + ((j % 2) * 2 + gi) * 4096, F32, HW_)

        def up_stage(hf, j):
            wget = WS.use("U%d_%d" % (hf, j))
            Ps = []
            for gi in range(2):
                pcol = ((bank_u[0] % 3) if bank_u[0] < 24 else (bank_u[0] % 4)) * 1024
                bank_u[0] += 1
                mms = []
                for n in range(2):
                    for k in range(8):
                        mms.append((B.ps(pcol + n * 512, 512), wget(k, gi * 128, gi * 128 + 128), h2v(k, n * 512, (n + 1) * 512),
                                    k == 0, k == 7))
                mm_group(mms, name="up%d_%d" % (j, gi))
                Ps.append(B.ps(pcol, HW_))
            WS.done("U%d_%d" % (hf, j))
            for gi in range(2):
                ch = j if gi == 0 else NFF + j
                P, t = Ps[gi], gvbuf(j, gi)
                halo = B.sb(HALO + ch * 8, F32, 2)
                w1, w0 = cv(C_FCW + 1 * 48 + ch), cv(C_FCW + 0 * 48 + ch)
                act(t, P, AF.Identity, bias=cv(C_FCB + ch), scale=cv(C_FCW + 2 * 48 + ch), name="fconv_t2")
                if hf == 0:
                    act(halo, P.sl(HW_ - 2, HW_), AF.Copy, name="halo")
                else:
                    act(t.sl(0, 1), halo.sl(1, 2), AF.Identity, bias=t.sl(0, 1), scale=w1, name="fconv_b1")
                    act(t.sl(0, 1), halo.sl(0, 1), AF.Identity, bias=t.sl(0, 1), scale=w0, name="fconv_b0a")
                    act(t.sl(1, 2), halo.sl(1, 2), AF.Identity, bias=t.sl(1, 2), scale=w0, name="fconv_b0b")
            for gi in range(2):
                ch = j if gi == 0 else NFF + j
                P, t = Ps[gi], gvbuf(j, gi)
                stt(t.sl(1, HW_), P.sl(0, HW_ - 1), cv(C_FCW + 1 * 48 + ch), t.sl(1, HW_), ALU.mult, ALU.add, name="fconv_t1")
                stt(t.sl(2, HW_), P.sl(0, HW_ - 2), cv(C_FCW + 0 * 48 + ch), t.sl(2, HW_), ALU.mult, ALU.add, name="fconv_t0")
            g, v = gvbuf(j, 0), gvbuf(j, 1)
            act(g, g, AF.Gelu_apprx_tanh, name="gelu_f")
            tt(B.sb(FB + j * 2048, BF16, HW_), g, v, ALU.mult, name="f")

        def fv(k, c0, c1):
            return B.sb(FB + k * 2048, BF16, HW_).sl(c0, c1)

        def odv(qq, d):
            return B.sb((OSB if qq == 0 else GV) + d * 2048, F32, QW)

        def down_stage(hf, order, mid_hook=None):
            ss3 = [B.ps(6 * 512, QW), B.ps(7 * 512, QW)]
            pend = None
            KS = 20
            seen_d = {}
            cnt = [0, 0]
            for (d, qq) in order:
                wget = WS.use("D%d_%d" % (hf, d))
                pcol = (bank_d[0] % 6) * 512
                bank_d[0] += 1
                P = B.ps(pcol, QW)
                mm_group([(P, wget(k, 0, 128), fv(k, qq * QW, (qq + 1) * QW), k == 0, False) for k in range(KS)],
                         name="down%d_%d a" % (d, qq))
                mm_group([(P, wget(k, 0, 128), fv(k, qq * QW, (qq + 1) * QW), False, k == NFF - 1) for k in range(KS, NFF)],
                         name="down%d_%d b" % (d, qq))
                if pend is not None:
                    mm_group([(ss3[pend[2]], ones_r, pend[0], pend[3] == 0, pend[3] == 7)], name="ss3mm")
                    if pend[3] == 7 and mid_hook is not None and pend[2] == 0:
                        mid_hook()
                act(odv(qq, d), P, AF.Identity, scale=cv(C_G4 + d), name="ocopy")
                pend = (sq_act(P), d, qq, cnt[qq])
                cnt[qq] += 1
                seen_d[d] = seen_d.get(d, 0) + 1
                if seen_d[d] == 2:
                    WS.done("D%d_%d" % (hf, d))
            mm_group([(ss3[pend[2]], ones_r, pend[0], pend[3] == 0, pend[3] == 7)], name="ss3mm")

        def post_rstd(qq, rs, tmp):
            ss3 = B.ps((6 + qq) * 512, QW)
            rstd_act(rs, tmp, ss3, name="rstdF")

        def post_apply(hf, qq, rs):
            q = 2 * hf + qq
            ga, gb = q * QW, (q + 1) * QW
            obase = OSB if qq == 0 else GV
            ng = 4 if q == 3 else 2
            per = 8 // ng
            for g in range(ng):
                og = B.sb3(obase + g * per * 2048, F32, per, QW, 0, QW)
                xg = B.sb3(RX + g * per * 8192, F32, per, S, ga, gb)
                tt(og, og, bcast(rs, per), ALU.mult, name="on")
                tt(xg, og, xg, ALU.add, name="out")
                dma("sp", yT3[:, g * per:(g + 1) * per, ga:gb], xg, "out%d" % q, name="store%d_%d" % (q, g), out_is_dram=True)
            B.out_chans.append(("out%d" % q, B.chan_val["out%d" % q]))

        ORDER0 = [(d, qq) for d in range(8) for qq in range(2)]
        ORDER1 = [(0, 0), (1, 0), (2, 0), (3, 0), (4, 0), (0, 1), (5, 0), (1, 1), (6, 0), (2, 1), (7, 0), (3, 1),
                  (4, 1), (5, 1), (6, 1), (7, 1)]
        rs_def = B.sb(E_RSTD2, F32, QW)
        rs_tmp = B.sb(E_RSTD2 + 2048, F32, QW)
        for hf in range(2):
            for j in range(NFF):
                up_stage(hf, j)
                if hf == 0 and 2 <= j < 10:
                    ss2_piece(3, j - 2)
                if hf == 0 and j == 10:
                    ss2_flush()
                    WS.open_gate("phaseO_done")
            if hf == 0:
                h2_stage(1)
                down_stage(0, ORDER0)
                post_rstd(1, rsF[0], sdF[0])
                post_rstd(0, rs_def, rs_tmp)
                post_apply(0, 1, rsF[0])
            else:
                WS.open_gate("up1_done")
                post_apply(0, 0, rs_def)

                def _mid():
                    post_rstd(0, rsF[0], sdF[0])
                    post_apply(1, 0, rsF[0])
                down_stage(1, ORDER1, mid_hook=_mid)
                post_rstd(1, rsF[0], sdF[0])
                post_apply(1, 1, rsF[0])

        assert HALO + 48 * 8 <= ARENA_BYTES

        def sem_of(key):
            kind, nm = key
            return sem_eng[nm] if kind == "eng" else chan_sems[nm]

        def emit_engine(e, name):
            for op in B.ops[name]:
                for key, val in op.waits:
                    e.wait_ge(sem_of(key), val)
                inst = op.emit(e)
                if op.is_dma:
                    inst.then_inc(chan_sems[op.chan], 16)
                else:
                    inst.then_inc(sem_eng[name], 1)

        if debug == "deps":
            return B
        block = es.enter_context(nc.Block())

        @block.sync
        def _(e):
            emit_engine(e, "sp")
            for c, v in B.out_chans:
                e.wait_ge(chan_sems[c], v)

        @block.gpsimd
        def _(e):
            emit_engine(e, "pool")

        @block.tensor
        def _(e):
            emit_engine(e, "pe")

        @block.scalar
        def _(e):
            emit_engine(e, "act")

        @block.vector
        def _(e):
            emit_engine(e, "dve")
    return nc


def _tile_w(w):
    K, C = w.shape
    return np.ascontiguousarray(w.reshape(K // 128, 128, C).transpose(1, 0, 2))


def _chunkcols(v, n):
    return np.ascontiguousarray(np.asarray(v, np.float32).reshape(n, 128).T)


def prepare_inputs(x, norm_mix_pre, norm_mix_post, norm_ffn_pre, norm_ffn_post, w_in, conv_short_w,
                   w_conv_branch, lru_conv_w, lru_conv_b, lru_wa, lru_ba, lru_wx, lru_bx, lru_lambda,
                   w_lru_branch, w_out, ffn_w_up, ffn_conv_w, ffn_conv_b, ffn_w_down):
    f = np.float32
    w_in = np.asarray(w_in[0], f)
    cb, cc, cx = w_in[:, 0:1024], w_in[:, 1024:2048], w_in[:, 2048:3072]
    lx, ly = w_in[:, 3072:4096], w_in[:, 4096:5120]
    gconv, glru = w_in[:, 5120:6144], w_in[:, 6144:7168]
    wcb = np.asarray(w_conv_branch[0], f)
    wlb = np.asarray(w_lru_branch[0], f)

    def ch(m, j):
        return m[:, j * 128:(j + 1) * 128]
    wA = np.stack([_tile_w(np.concatenate([ch(cx, j), ch(cc, j), ch(cb, j)], axis=1)).reshape(128, -1) for j in range(8)])
    wB = np.stack([_tile_w(np.concatenate([ch(lx, 2 * h), ch(lx, 2 * h + 1), ch(ly, 2 * h), ch(ly, 2 * h + 1)], axis=1)).reshape(128, -1)
                   for h in range(4)])
    wa = np.asarray(lru_wa[0], f)
    wx = np.asarray(lru_wx[0], f)
    wG = np.stack([_tile_w(np.concatenate([wa[h], wx[h]], axis=1)).reshape(128, -1) for h in range(4)])
    wM = np.stack([_tile_w(np.concatenate([ch(gconv, j), ch(wcb, j), ch(glru, j), ch(wlb, j)], axis=1)).reshape(128, -1)
                   for j in range(8)])
    wO = _tile_w(np.asarray(w_out[0], f)).reshape(128, -1)
    wup = np.asarray(ffn_w_up[0], f)
    gate, val = wup[:, 0:3072], wup[:, 3072:6144]
    wU = np.stack([_tile_w(np.concatenate([ch(gate, j), ch(val, j)], axis=1)).reshape(128, -1) for j in range(24)])
    wdn = np.asarray(ffn_w_down[0], f)
    wD = np.stack([_tile_w(ch(wdn, d)).reshape(128, -1) for d in range(8)])

    cvec = np.zeros((128, NV), f)
    cvec[:, C_G1:C_G1 + 8] = _chunkcols(norm_mix_pre[0], 8)
    cvec[:, C_G2:C_G2 + 8] = _chunkcols(norm_mix_post[0], 8)
    cvec[:, C_G3:C_G3 + 8] = _chunkcols(norm_ffn_pre[0], 8)
    cvec[:, C_G4:C_G4 + 8] = _chunkcols(norm_ffn_post[0], 8)
    for t in range(3):
        cvec[:, C_CSW + t * 8:C_CSW + t * 8 + 8] = _chunkcols(conv_short_w[0][t], 8)
    for t in range(4):
        cvec[:, C_LCW + t * 8:C_LCW + t * 8 + 8] = _chunkcols(lru_conv_w[0][t], 8)
    cvec[:, C_LCB:C_LCB + 8] = _chunkcols(lru_conv_b[0], 8)
    cvec[:, C_LBA:C_LBA + 8] = _chunkcols(np.asarray(lru_ba[0]).reshape(-1), 8)
    cvec[:, C_LBX:C_LBX + 8] = _chunkcols(np.asarray(lru_bx[0]).reshape(-1), 8)
    cvec[:, C_LAM:C_LAM + 8] = _chunkcols(lru_lambda[0], 8)
    for t in range(3):
        cvec[:, C_FCW + t * 48:C_FCW + t * 48 + 48] = _chunkcols(ffn_conv_w[0][t], 48)
    cvec[:, C_FCB:C_FCB + 48] = _chunkcols(ffn_conv_b[0], 48)
    shared = dict(cvec=cvec, wA=wA, wB=wB, wG=wG, wM=wM, wO=wO, wU=wU, wD=wD)
    x = np.asarray(x, f)
    in_maps = []
    for b in range(NCORES):
        m = dict(shared)
        m["xT"] = np.ascontiguousarray(x[b].T)
        in_maps.append(m)
    return in_maps


def kernel(**inputs):
    in_maps = prepare_inputs(**inputs)
    nc = build_program()
    res = run_bass_kernel_spmd(nc, in_maps, core_ids=list(range(NCORES)))
    out = np.stack([np.ascontiguousarray(res.results[b]["yT"].T) for b in range(NCORES)], axis=0)
    return out.astype(np.float32)
```
